# Optimizing a Trainium2 kernel written in Bass

```python
import math
import jax
import jax.numpy as jnp
from jax import lax
import numpy as np

D_MODEL = 1024
BATCH = 4
SEQ = 8192
DEPTH = 2

GRID_W = 64
CTX_LEN = 256

SSM_GROUP_CH = 16
SSM_GROUPS = 16
SSM_WIDTH = SSM_GROUPS * SSM_GROUP_CH
SSM_STATE = 64
DT_MIN = 0.001
DT_MAX = 0.1
FNET_GROUPS = 4
FNET_GROUP_CH = 64
FNET_WIDTH = FNET_GROUPS * FNET_GROUP_CH
NA_HEADS = 8
NA_HEAD_DIM = 64
NA_WIDTH = NA_HEADS * NA_HEAD_DIM
WIN_ROWS = 8
WIN_COLS = 16
ROPE_THETA = 10000.0
N_BRANCH = 3
D_FF = 2816
N_MOD = 9
LN_EPS = 1e-6
DN_ALPHA = (2 * DEPTH) ** 0.25
DN_BETA = (8 * DEPTH) ** -0.25

COL_SSM = 0
COL_K = COL_SSM + SSM_WIDTH
COL_V = COL_K + NA_WIDTH
COL_CTX_END = COL_V + NA_WIDTH
COL_FNET = COL_CTX_END
COL_Q = COL_FNET + FNET_WIDTH
COL_GATE = COL_Q + NA_WIDTH
W_IN = COL_GATE + N_BRANCH * D_MODEL

kernel_name = 'hybrid_s5_fnet_natten_macaron_deepnorm'


def _ln(x):
    xf = x.astype(jnp.float32)
    mu = jnp.mean(xf, axis=-1, keepdims=True)
    var = jnp.mean(jnp.square(xf - mu), axis=-1, keepdims=True)
    return ((xf - mu) * lax.rsqrt(var + LN_EPS)).astype(x.dtype)


def _post_norm(x, res, g, b):
    return _ln(DN_ALPHA * x + res) * g + b


def _modulate(x, shift, scale):
    return _ln(x) * (1.0 + scale) + shift


def _swiglu(u, w_gate, w_up, w_down):
    return (jax.nn.silu(u @ w_gate) * (u @ w_up)) @ w_down


def _heads(t):
    return t.reshape(t.shape[0], t.shape[1], NA_HEADS, NA_HEAD_DIM)


def _cmul(ar, ai, br, bi):
    return ar * br - ai * bi, ar * bi + ai * br


def _scan_combine(e1, e2):
    a1r, a1i, b1r, b1i = e1
    a2r, a2i, b2r, b2i = e2
    ar, ai = _cmul(a2r, a2i, a1r, a1i)
    br, bi = _cmul(a2r, a2i, b1r, b1i)
    return ar, ai, br + b2r, bi + b2i


def _zoh(log_dt, a_re, a_im, b_re, b_im):
    dt = jnp.exp(log_dt.astype(jnp.float32))[:, None]
    a_re = a_re.astype(jnp.float32)
    a_im = a_im.astype(jnp.float32)
    mag = jnp.exp(a_re * dt)
    ab_re = mag * jnp.cos(a_im * dt)
    ab_im = mag * jnp.sin(a_im * dt)
    den = a_re * a_re + a_im * a_im
    nr = ab_re - 1.0
    ni = ab_im
    fr = (nr * a_re + ni * a_im) / den
    fi = (ni * a_re - nr * a_im) / den
    b_re = b_re.astype(jnp.float32)
    b_im = b_im.astype(jnp.float32)
    bb_re = fr[..., None] * b_re - fi[..., None] * b_im
    bb_im = fr[..., None] * b_im + fi[..., None] * b_re
    return ab_re, ab_im, bb_re, bb_im


def _s5_scan(u, disc, h0, reverse):
    ab_re, ab_im, bb_re, bb_im = disc
    uf = u.astype(jnp.float32)
    bu_re = jnp.einsum('blgc,gpc->blgp', uf, bb_re)
    bu_im = jnp.einsum('blgc,gpc->blgp', uf, bb_im)
    if h0 is not None:
        ah_re, ah_im = _cmul(ab_re, ab_im, h0[0], h0[1])
        first = -1 if reverse else 0
        bu_re = bu_re.at[:, first].add(ah_re)
        bu_im = bu_im.at[:, first].add(ah_im)
    a_re = jnp.broadcast_to(ab_re, bu_re.shape)
    a_im = jnp.broadcast_to(ab_im, bu_re.shape)
    _, _, h_re, h_im = lax.associative_scan(_scan_combine, (a_re, a_im, bu_re, bu_im), reverse=reverse, axis=1)
    return h_re, h_im


def _s5_readout(u, h_fwd, h_bwd, c_re, c_im, d, w_glu, b_glu):
    B, L, _ = u.shape
    cr = c_re.astype(jnp.float32)
    ci = c_im.astype(jnp.float32)
    y = (jnp.einsum('blgp,gcp->blgc', h_fwd[0], cr[0]) - jnp.einsum('blgp,gcp->blgc', h_fwd[1], ci[0])
         + jnp.einsum('blgp,gcp->blgc', h_bwd[0], cr[1]) - jnp.einsum('blgp,gcp->blgc', h_bwd[1], ci[1]))
    y = y.reshape(B, L, SSM_WIDTH).astype(u.dtype) + d * u
    y = jax.nn.gelu(y)
    return y * jax.nn.sigmoid(y @ w_glu + b_glu)


def _fourier_mix(f):
    B, L, _ = f.shape
    fg = f.reshape(B, L, FNET_GROUPS, FNET_GROUP_CH).astype(jnp.float32)
    y = jnp.fft.fft2(fg, axes=(1, 3), norm='ortho').real
    return y.reshape(B, L, FNET_WIDTH).astype(f.dtype)


def _axial_rope(x):
    B, L, H, hd = x.shape
    nf = hd // 4
    t = jnp.arange(L, dtype=jnp.int32)
    pos = jnp.stack([t // GRID_W, t % GRID_W], axis=-1).astype(jnp.float32)
    inv_freq = ROPE_THETA ** (-jnp.arange(nf, dtype=jnp.float32) / nf)
    ang = pos[:, :, None] * inv_freq
    cos = jnp.cos(ang)[None, :, None].astype(x.dtype)
    sin = jnp.sin(ang)[None, :, None].astype(x.dtype)
    xr = x.reshape(B, L, H, 2, 2, nf)
    x1 = xr[..., 0, :]
    x2 = xr[..., 1, :]
    out = jnp.stack([x1 * cos - x2 * sin, x2 * cos + x1 * sin], axis=-2)
    return out.reshape(B, L, H, hd)


def _neighbourhood_attention(q, k, v, kc, vc, rpb):
    B, L, H, hd = q.shape
    rows = L // GRID_W
    wr = min(WIN_ROWS, rows)
    wc = WIN_COLS
    scale = hd ** -0.5
    qg = q.reshape(B, rows, GRID_W, H, hd)
    kg = k.reshape(B, rows, GRID_W, H, hd)
    vg = v.reshape(B, rows, GRID_W, H, hd)
    col = jnp.arange(GRID_W, dtype=jnp.int32)
    col_start = jnp.clip(col - wc // 2, 0, GRID_W - wc)
    col_idx = col_start[:, None] + jnp.arange(wc, dtype=jnp.int32)[None, :]
    dc_idx = col_idx - col[:, None] + (WIN_COLS - 1)

    def row_block(r):
        r0 = jnp.clip(r - wr // 2, 0, rows - wr)
        k_rows = lax.dynamic_slice_in_dim(kg, r0, wr, axis=1)
        v_rows = lax.dynamic_slice_in_dim(vg, r0, wr, axis=1)
        k_win = k_rows[:, :, col_idx].transpose(0, 2, 1, 3, 4, 5).reshape(B, GRID_W, wr * wc, H, hd)
        v_win = v_rows[:, :, col_idx].transpose(0, 2, 1, 3, 4, 5).reshape(B, GRID_W, wr * wc, H, hd)
        q_r = lax.dynamic_index_in_dim(qg, r, axis=1, keepdims=False)
        dr_idx = r0 + jnp.arange(wr, dtype=jnp.int32) - r + (WIN_ROWS - 1)
        bias = rpb[:, dr_idx[None, :, None], dc_idx[:, None, :]].reshape(H, GRID_W, wr * wc)
        s_loc = jnp.einsum('bqhd,bqkhd->bhqk', q_r, k_win).astype(jnp.float32) * scale + bias[None].astype(jnp.float32)
        s_ctx = jnp.einsum('bqhd,bkhd->bhqk', q_r, kc).astype(jnp.float32) * scale
        p = jax.nn.softmax(jnp.concatenate([s_loc, s_ctx], axis=-1), axis=-1).astype(v.dtype)
        n_loc = wr * wc
        out = (jnp.einsum('bhqk,bqkhd->bqhd', p[..., :n_loc], v_win)
               + jnp.einsum('bhqk,bkhd->bqhd', p[..., n_loc:], vc))
        return out

    out = lax.map(row_block, jnp.arange(rows, dtype=jnp.int32))
    return out.transpose(1, 0, 2, 3, 4).reshape(B, L, H * hd)


def _context_attention(qc, kc, vc):
    B, Lc, H, hd = qc.shape
    s = jnp.einsum('bqhd,bkhd->bhqk', qc, kc).astype(jnp.float32) * (hd ** -0.5)
    p = jax.nn.softmax(s, axis=-1).astype(vc.dtype)
    return jnp.einsum('bhqk,bkhd->bqhd', p, vc).reshape(B, Lc, H * hd)


def _merge(y_s, y_f, y_n, gates, w_br_ssm, w_br_fnet, w_br_na, w_out):
    B, L, _ = gates.shape
    g = jax.nn.sigmoid(gates).reshape(B, L, N_BRANCH, D_MODEL)
    y = (g[:, :, 0] * (y_s @ w_br_ssm) + g[:, :, 1] * (y_f @ w_br_fnet) + g[:, :, 2] * (y_n @ w_br_na))
    return y @ w_out


def _mixer(u, uc, w_in, log_dt, a_re, a_im, b_re, b_im, c_re, c_im, d, w_glu, b_glu, rpb,
           w_br_ssm, w_br_fnet, w_br_na, w_out, ctx_out):
    B, L, _ = u.shape
    Lc = uc.shape[1]
    z = u @ w_in
    zc = uc @ (w_in if ctx_out else w_in[:, :COL_CTX_END])
    disc_f = _zoh(log_dt[0], a_re[0], a_im[0], b_re[0], b_im[0])
    disc_b = _zoh(log_dt[1], a_re[1], a_im[1], b_re[1], b_im[1])
    us = z[..., COL_SSM:COL_K]
    usc = zc[..., COL_SSM:COL_K]
    usg = us.reshape(B, L, SSM_GROUPS, SSM_GROUP_CH)
    uscg = usc.reshape(B, Lc, SSM_GROUPS, SSM_GROUP_CH)
    hcf = _s5_scan(uscg, disc_f, None, False)
    hcb = _s5_scan(uscg, disc_b, None, True)
    hf = _s5_scan(usg, disc_f, (hcf[0][:, -1], hcf[1][:, -1]), False)
    hb = _s5_scan(usg, disc_b, (hcb[0][:, 0], hcb[1][:, 0]), True)
    y_s = _s5_readout(us, hf, hb, c_re, c_im, d, w_glu, b_glu)
    y_f = _fourier_mix(z[..., COL_FNET:COL_Q])
    q = _axial_rope(_heads(z[..., COL_Q:COL_GATE]))
    k = _axial_rope(_heads(z[..., COL_K:COL_V]))
    v = _heads(z[..., COL_V:COL_CTX_END])
    kc = _heads(zc[..., COL_K:COL_V])
    vc = _heads(zc[..., COL_V:COL_CTX_END])
    y_n = _neighbourhood_attention(q, k, v, kc, vc, rpb)
    y = _merge(y_s, y_f, y_n, z[..., COL_GATE:], w_br_ssm, w_br_fnet, w_br_na, w_out)
    if not ctx_out:
        return y, None
    y_s_c = _s5_readout(usc, hcf, hcb, c_re, c_im, d, w_glu, b_glu)
    y_f_c = _fourier_mix(zc[..., COL_FNET:COL_Q])
    y_n_c = _context_attention(_heads(zc[..., COL_Q:COL_GATE]), kc, vc)
    yc = _merge(y_s_c, y_f_c, y_n_c, zc[..., COL_GATE:], w_br_ssm, w_br_fnet, w_br_na, w_out)
    return y, yc


def setup_inputs(seed: int = 0) -> dict:
    key = jax.random.key(seed)
    ks = jax.random.split(key, 27)
    f32 = jnp.float32

    def nrm(k, shape, scale):
        return jax.random.normal(k, shape, f32) * scale

    G, P, C = SSM_GROUPS, SSM_STATE, SSM_GROUP_CH
    n = jnp.arange(P, dtype=f32)
    return {
        'x': nrm(ks[0], (BATCH, SEQ, D_MODEL), 1.0),
        'c': nrm(ks[1], (BATCH, D_MODEL), 1.0),
        'ctx': nrm(ks[2], (BATCH, CTX_LEN, D_MODEL), 1.0),
        'c_ctx': nrm(ks[3], (D_MODEL,), 1.0),
        'w_mod': nrm(ks[4], (DEPTH, D_MODEL, N_MOD * D_MODEL), D_MODEL ** -0.5),
        'b_mod': nrm(ks[5], (DEPTH, N_MOD * D_MODEL), 0.02),
        'ln_g': 1.0 + nrm(ks[6], (DEPTH, 3, D_MODEL), 0.02),
        'ln_b': nrm(ks[7], (DEPTH, 3, D_MODEL), 0.02),
        'ffn_w_gate': nrm(ks[8], (DEPTH, 2, D_MODEL, D_FF), D_MODEL ** -0.5),
        'ffn_w_up': nrm(ks[9], (DEPTH, 2, D_MODEL, D_FF), D_MODEL ** -0.5),
        'ffn_w_down': nrm(ks[10], (DEPTH, 2, D_FF, D_MODEL), DN_BETA * D_FF ** -0.5),
        'w_in': nrm(ks[11], (DEPTH, D_MODEL, W_IN), D_MODEL ** -0.5),
        'ssm_log_dt': jax.random.uniform(ks[12], (DEPTH, 2, G), f32, math.log(DT_MIN), math.log(DT_MAX)),
        'ssm_a_re': -0.5 + nrm(ks[13], (DEPTH, 2, G, P), 0.01),
        'ssm_a_im': jnp.pi * n + nrm(ks[14], (DEPTH, 2, G, P), 0.01),
        'ssm_b_re': nrm(ks[15], (DEPTH, 2, G, P, C), (2 * C) ** -0.5),
        'ssm_b_im': nrm(ks[16], (DEPTH, 2, G, P, C), (2 * C) ** -0.5),
        'ssm_c_re': nrm(ks[17], (DEPTH, 2, G, C, P), P ** -0.5),
        'ssm_c_im': nrm(ks[18], (DEPTH, 2, G, C, P), P ** -0.5),
        'ssm_d': 1.0 + nrm(ks[19], (DEPTH, SSM_WIDTH), 0.1),
        'ssm_w_glu': nrm(ks[20], (DEPTH, SSM_WIDTH, SSM_WIDTH), SSM_WIDTH ** -0.5),
        'ssm_b_glu': nrm(ks[21], (DEPTH, SSM_WIDTH), 0.02),
        'na_rpb': nrm(ks[22], (DEPTH, NA_HEADS, 2 * WIN_ROWS - 1, 2 * WIN_COLS - 1), 0.02),
        'w_br_ssm': nrm(ks[23], (DEPTH, SSM_WIDTH, D_MODEL), SSM_WIDTH ** -0.5),
        'w_br_fnet': nrm(ks[24], (DEPTH, FNET_WIDTH, D_MODEL), FNET_WIDTH ** -0.5),
        'w_br_na': nrm(ks[25], (DEPTH, NA_WIDTH, D_MODEL), NA_WIDTH ** -0.5),
        'w_out': nrm(ks[26], (DEPTH, D_MODEL, D_MODEL), DN_BETA * D_MODEL ** -0.5),
    }


def reference(x, c, ctx, c_ctx, w_mod, b_mod, ln_g, ln_b, ffn_w_gate, ffn_w_up, ffn_w_down, w_in,
              ssm_log_dt, ssm_a_re, ssm_a_im, ssm_b_re, ssm_b_im, ssm_c_re, ssm_c_im, ssm_d,
              ssm_w_glu, ssm_b_glu, na_rpb, w_br_ssm, w_br_fnet, w_br_na, w_out):
    h, hc = x, ctx
    for l in range(DEPTH):
        last = l == DEPTH - 1
        mod = jax.nn.silu(c) @ w_mod[l] + b_mod[l]
        mod_c = jax.nn.silu(c_ctx) @ w_mod[l] + b_mod[l]
        m = [mod[:, None, i * D_MODEL:(i + 1) * D_MODEL] for i in range(N_MOD)]
        mc = [mod_c[i * D_MODEL:(i + 1) * D_MODEL] for i in range(N_MOD)]
        h = _post_norm(h, 0.5 * m[2] * _swiglu(_modulate(h, m[0], m[1]), ffn_w_gate[l, 0], ffn_w_up[l, 0], ffn_w_down[l, 0]),
                       ln_g[l, 0], ln_b[l, 0])
        hc = _post_norm(hc, 0.5 * mc[2] * _swiglu(_modulate(hc, mc[0], mc[1]), ffn_w_gate[l, 0], ffn_w_up[l, 0], ffn_w_down[l, 0]),
                        ln_g[l, 0], ln_b[l, 0])
        y, yc = _mixer(_modulate(h, m[3], m[4]), _modulate(hc, mc[3], mc[4]), w_in[l],
                       ssm_log_dt[l], ssm_a_re[l], ssm_a_im[l], ssm_b_re[l], ssm_b_im[l], ssm_c_re[l], ssm_c_im[l],
                       ssm_d[l], ssm_w_glu[l], ssm_b_glu[l], na_rpb[l], w_br_ssm[l], w_br_fnet[l], w_br_na[l], w_out[l],
                       not last)
        h = _post_norm(h, m[5] * y, ln_g[l, 1], ln_b[l, 1])
        h = _post_norm(h, 0.5 * m[8] * _swiglu(_modulate(h, m[6], m[7]), ffn_w_gate[l, 1], ffn_w_up[l, 1], ffn_w_down[l, 1]),
                       ln_g[l, 2], ln_b[l, 2])
        if not last:
            hc = _post_norm(hc, mc[5] * yc, ln_g[l, 1], ln_b[l, 1])
            hc = _post_norm(hc, 0.5 * mc[8] * _swiglu(_modulate(hc, mc[6], mc[7]), ffn_w_gate[l, 1], ffn_w_up[l, 1], ffn_w_down[l, 1]),
                            ln_g[l, 2], ln_b[l, 2])
    return h
```

```python
import numpy as np
import ml_dtypes
from contextlib import ExitStack
import concourse.bass as bass
import concourse.mybir as mybir
from concourse.bass_utils import run_bass_kernel_spmd

F32 = mybir.dt.float32
BF16 = mybir.dt.bfloat16
AF = mybir.ActivationFunctionType
ALU = mybir.AluOpType
AX = mybir.AxisListType
NPBF = ml_dtypes.bfloat16

D = 1024; DFF = 2816; NF = 22; L = 8192; LC = 256; NB = 4; DEPTH = 2
GRID_W = 64
LN_EPS = 1e-6
DN_ALPHA = (2 * DEPTH) ** 0.25
COL_SSM = 0; COL_K = 256; COL_V = 768; COL_FNET = 1280; COL_Q = 1536; COL_GATE = 2048


class Sched:
    def __init__(self, nc, es):
        self.nc = nc; self.es = es
        self.E = {'pe': nc.tensor, 'act': nc.scalar, 'dve': nc.vector, 'pool': nc.gpsimd, 'sp': nc.sync}
        self.sems = {}
        for e in ['pe', 'act', 'dve', 'pool']:
            self.sems[e] = es.enter_context(nc.semaphore("s_" + e))
        self.cnt = {e: 0 for e in ['pe', 'act', 'dve', 'pool']}
        self.NR = 8
        self.dn = {}
        for q in ['sp', 'pool', 'act']:
            self.dn[q] = 0
            for i in range(self.NR):
                self.sems["d_%s_%d" % (q, i)] = es.enter_context(nc.semaphore("d_%s_%d" % (q, i)))
        self.lastw = {}
        self.readers = {}
        self.seen = {e: {} for e in self.E}
        self.nps = 0
        self.psum = [es.enter_context(nc.psum_tensor("ps%d" % i, [128, 512], F32)) for i in range(8)]
        self.out_handles = []

    def sb(self, name, shape, dt):
        return self.es.enter_context(self.nc.sbuf_tensor(name, list(shape), dt))

    def ps(self):
        i = self.nps % 8; self.nps += 1
        return self.psum[i], ("ps", i)

    def _wait(self, eng, deps):
        E = self.E[eng]
        best = {}
        for (sn, v) in deps:
            if eng == 'pe' and sn == 'pe':
                continue
            if best.get(sn, 0) < v:
                best[sn] = v
        for sn, v in best.items():
            if self.seen[eng].get(sn, 0) >= v:
                continue
            E.wait_ge(self.sems[sn], v)
            self.seen[eng][sn] = v

    def _deps(self, r, w):
        deps = []
        for k in r:
            if k in self.lastw:
                deps.append(self.lastw[k])
        for k in w:
            if k in self.lastw:
                deps.append(self.lastw[k])
            for sn, v in self.readers.get(k, {}).items():
                deps.append((sn, v))
        return deps

    def _commit(self, h, r, w):
        for k in w:
            self.lastw[k] = h
            self.readers[k] = {}
        for k in r:
            d = self.readers.setdefault(k, {})
            if d.get(h[0], 0) < h[1]:
                d[h[0]] = h[1]

    def op(self, eng, fn, r=(), w=()):
        self._wait(eng, self._deps(r, w))
        ins = fn(self.E[eng])
        self.cnt[eng] += 1
        ins.then_inc(self.sems[eng], 1)
        h = (eng, self.cnt[eng])
        self._commit(h, r, w)
        return h

    def dma(self, q, out, in_, r=(), w=(), **kw):
        n = self.dn[q]; self.dn[q] += 1
        sn = "d_%s_%d" % (q, n % self.NR)
        deps = self._deps(r, w)
        if n >= self.NR:
            deps.append((sn, 16 * (n // self.NR)))
        self._wait(q, deps)
        ins = self.E[q].dma_start(out=out, in_=in_, **kw)
        ins.then_inc(self.sems[sn], 16)
        h = (sn, 16 * (n // self.NR + 1))
        self._commit(h, r, w)
        return h

    def finish(self, keys):
        deps = []
        for k in keys:
            if k in self.lastw:
                deps.append(self.lastw[k])
        self._wait('sp', deps)


def mm(S, out, lhsT, rhs, start, stop, r, w):
    return S.op('pe', lambda E: E.matmul(out, lhsT=lhsT, rhs=rhs, start=start, stop=stop), r=r, w=w)


class TokCtx:
    pass


def setup_common(S, nc, dr, need_mods, lidx_ln, Bf):
    C = TokCtx()
    fm, tm, tmscale = need_mods
    ident_f = S.sb("ident_f", [128, 128], F32)
    C.ident = S.sb("ident", [128, 128], BF16)
    S.op('pool', lambda E: E.memset(ident_f[:], 1.0), w=["ident_f"])
    S.op('pool', lambda E: E.affine_select(out=ident_f[:], in_=ident_f[:], pattern=[[-1, 128]], compare_op=ALU.is_equal,
                                           fill=0.0, base=0, channel_multiplier=1), r=["ident_f"], w=["ident_f"])
    S.op('dve', lambda E: E.tensor_copy(out=C.ident[:], in_=ident_f[:]), r=["ident_f"], w=["ident"])
    cT = S.sb("cT", [128, 8, 2], F32)
    S.dma('sp', cT[:, :, 0], dr['c_row'].rearrange("(k p) -> p k", p=128), w=["cT"], allow_slow_non_contiguous=True)
    S.dma('sp', cT[:, :, 1], dr['c_ctx'].rearrange("(k p) -> p k", p=128), w=["cT"], allow_slow_non_contiguous=True)
    sT = S.sb("sT", [128, 8, 2], BF16)
    S.op('act', lambda E: E.activation(out=sT[:], in_=cT[:], func=AF.Silu), r=["cT"], w=["sT"])
    sR = S.sb("sR", [128, 8, 2, 128], BF16)
    for k in range(8):
        for j in range(2):
            S.op('dve', lambda E, k=k, j=j: E.tensor_copy(out=sR[:, k, j, :], in_=sT[:, k, j:j + 1].to_broadcast([128, 128])),
                 r=["sT"], w=["sR"])
    bT = S.sb("bT", [128, 72], F32)
    S.dma('sp', bT[:], dr['b_mod'].rearrange("(c p) -> p c", p=128), w=["bT"], allow_slow_non_contiguous=True)
    C.modT = S.sb("modT", [128, 72, 2], F32)
    C.modB = {}
    for i in tm:
        for j in range(2):
            C.modB[(i, j)] = S.sb("modB_%d_%d" % (i, j), [128, 1024], F32)
    bB = S.sb("bB", [128, 1024], F32)
    wm = Bf.wm
    wmod_v = dr['w_mod'].rearrange("(k p) c -> p k c", p=128)
    nch = 0
    for i in range(9):
        if i not in fm and i not in tm:
            continue
        for hf in range(2):
            wt = wm[nch % 2]; wk = ("wd", nch % 2); nch += 1
            c0 = i * 1024 + hf * 512
            S.dma('pool', wt, wmod_v[:, :, c0:c0 + 512], w=[wk])
            if i in fm:
                for kk in range(4):
                    pt, pk = S.ps()
                    for k in range(8):
                        mm(S, pt[:, 0:2], wt[:, k, kk * 128:(kk + 1) * 128], sT[:, k, :], k == 0, k == 7, [wk, "sT"], [pk])
                    col = i * 8 + hf * 4 + kk
                    S.op('dve', lambda E, pt=pt, col=col: E.tensor_tensor(out=C.modT[:, col, :], in0=pt[:, 0:2],
                                                                        in1=bT[:, col:col + 1].to_broadcast([128, 2]), op=ALU.add),
                         r=[pk, "bT"], w=["modT"])
            if i in tm:
                S.dma('sp', bB[:, 0:512], dr['b_mod'][c0:c0 + 512].partition_broadcast(128), w=["bB"])
                for j in range(2):
                    pt, pk = S.ps()
                    for k in range(8):
                        mm(S, pt[:], sR[:, k, j, :], wt[:, k, :], k == 0, k == 7, [wk, "sR"], [pk])
                    dst = C.modB[(i, j)]
                    S.op('dve', lambda E, pt=pt, dst=dst, hf=hf: E.tensor_tensor(out=dst[:, hf * 512:(hf + 1) * 512], in0=pt[:],
                                                                               in1=bB[:, 0:512], op=ALU.add),
                         r=[pk, "bB"], w=[dst.name])
                    if tmscale != 1.0:
                        S.op('dve', lambda E, dst=dst, hf=hf: E.tensor_scalar(out=dst[:, hf * 512:(hf + 1) * 512],
                                                                            in0=dst[:, hf * 512:(hf + 1) * 512], scalar1=float(tmscale),
                                                                            scalar2=None, op0=ALU.mult),
                             r=[dst.name], w=[dst.name])
    for i in fm:
        if i % 3 == 1:
            S.op('dve', lambda E, i=i: E.tensor_scalar(out=C.modT[:, i * 8:(i + 1) * 8, :], in0=C.modT[:, i * 8:(i + 1) * 8, :],
                                                      scalar1=1.0, scalar2=None, op0=ALU.add), r=["modT"], w=["modT"])
    C.lnG = {}; C.lnB = {}
    for j in lidx_ln:
        C.lnG[j] = S.sb("lnG%d" % j, [128, 1024], F32)
        C.lnB[j] = S.sb("lnB%d" % j, [128, 1024], F32)
        S.dma('sp', C.lnG[j][:], dr['ln_g'][j].partition_broadcast(128), w=[C.lnG[j].name])
        S.dma('sp', C.lnB[j][:], dr['ln_b'][j].partition_broadcast(128), w=[C.lnB[j].name])
    return C


NT_MAX = 9
BLOCKS = [(0, 9), (9, 17), (17, 25), (25, 33)]
NTILES = 33
TTOK = NTILES * 128


def block_chunks(t0, t1):
    ch = []
    c = 0
    if t0 == 0:
        ch.append((0, 128, 1)); c = 128
    n = (t1 - t0) * 128
    while c < n:
        ch.append((c, min(c + 512, n), 0)); c = min(c + 512, n)
    return ch


class TokBufs:
    pass


def alloc_tok_bufs(S, NT_MAX=NT_MAX, ffn=True):
    Bf = TokBufs()
    Bf.hblk = S.sb("hblk", [128, NT_MAX, 1024], F32)
    Bf.uT = S.sb("uT", [128, 8, NT_MAX * 128], BF16)
    if ffn:
        Bf.aT = S.sb("aT", [128, NF, NT_MAX * 128], BF16)
    Bf.wd = S.sb("wd", [128, NF if ffn else 16, 512], BF16)
    Bf.wm = [Bf.wd[:, 0:8, :], Bf.wd[:, 11:19, :] if ffn else Bf.wd[:, 8:16, :]]
    Bf.wb = [S.sb("wb%d" % i, [128, 8, 256], BF16) for i in range(4)]
    Bf.xn = [S.sb("xn%d" % i, [128, 1024], BF16) for i in range(2)]
    Bf.t1 = S.sb("t1", [128, 1024], F32)
    Bf.sg = [S.sb("sg%d" % i, [128, 512], BF16) for i in range(2)]
    Bf.stats = [S.sb("stats%d" % i, [128, 2, 6], F32) for i in range(2)]
    Bf.mv = S.sb("mv", [128, NT_MAX, 2], F32)
    Bf.rstd = S.sb("rstd", [128, NT_MAX], F32)
    Bf.nmr = S.sb("nmr", [128, NT_MAX], F32)
    Bf.eps = S.sb("eps", [128, 1], F32)
    S.op('pool', lambda E: E.memset(Bf.eps[:], LN_EPS), w=["eps"])
    Bf.nx = 0; Bf.nsg = 0; Bf.nst = 0; Bf.nwb = 0
    return Bf


def ln_stats(S, Bf, src, srckeys, i):
    st = Bf.stats[Bf.nst % 2]; sk = st.name; Bf.nst += 1
    for hf in range(2):
        S.op('dve', lambda E, hf=hf: E.bn_stats(out=st[:, hf, :], in_=src[:, hf * 512:(hf + 1) * 512]), r=srckeys, w=[(sk, hf)])
    S.op('dve', lambda E: E.bn_aggr(out=Bf.mv[:, i, :], in_=st[:]), r=[(sk, 0), (sk, 1)], w=[("mv", i)])


def ln_finish(S, Bf, nt):
    rk = [("mv", i) for i in range(nt)]
    S.op('act', lambda E: E.activation(out=Bf.rstd[:, 0:nt], in_=Bf.mv[:, 0:nt, 1], func=AF.Sqrt, bias=Bf.eps[:, 0:1], scale=1.0),
         r=rk + ["eps"], w=["rstd"])
    S.op('dve', lambda E: E.reciprocal(out=Bf.rstd[:, 0:nt], in_=Bf.rstd[:, 0:nt]), r=["rstd"], w=["rstd"])
    S.op('dve', lambda E: E.scalar_tensor_tensor(out=Bf.nmr[:, 0:nt], in0=Bf.mv[:, 0:nt, 0], scalar=-1.0, in1=Bf.rstd[:, 0:nt],
                                                 op0=ALU.mult, op1=ALU.mult), r=rk + ["rstd"], w=["nmr"])


def modulate_T(S, C, Bf, i, is_ctx, i_shift, i_scale):
    xn = Bf.xn[Bf.nx % 2]; xk = xn.name; Bf.nx += 1
    S.op('act', lambda E: E.activation(out=xn[:], in_=Bf.hblk[:, i, :], func=AF.Identity, bias=Bf.nmr[:, i:i + 1], scale=Bf.rstd[:, i:i + 1]),
         r=[("hblk", i), "rstd", "nmr"], w=[xk])
    pt, pk = S.ps()
    ptb = pt.bitcast(BF16)
    for k in range(8):
        S.op('pe', lambda E, k=k: E.transpose(out=ptb[:, k * 128:(k + 1) * 128], in_=xn[:, k * 128:(k + 1) * 128], identity=C.ident[:]),
             r=[xk, "ident"], w=[pk])
    for k in range(8):
        dst = Bf.uT[:, k, i * 128:(i + 1) * 128]
        sc = C.modT[:, i_scale * 8 + k, is_ctx:is_ctx + 1]; sh = C.modT[:, i_shift * 8 + k, is_ctx:is_ctx + 1]
        if k % 2 == 0:
            S.op('dve', lambda E, k=k, dst=dst, sc=sc, sh=sh: E.tensor_scalar(out=dst, in0=ptb[:, k * 128:(k + 1) * 128], scalar1=sc, scalar2=sh,
                                                                             op0=ALU.mult, op1=ALU.add), r=[pk, "modT"], w=[("uT", i)])
        else:
            S.op('act', lambda E, k=k, dst=dst, sc=sc, sh=sh: E.activation(out=dst, in_=ptb[:, k * 128:(k + 1) * 128], func=AF.Identity, bias=sh, scale=sc),
                 r=[pk, "modT"], w=[("uT", i)])


def load_w(S, Bf, src_ap):
    wt = Bf.wb[Bf.nwb % 4]; Bf.nwb += 1
    S.dma('pool', wt[:], src_ap.rearrange("(k p) c -> p k c", p=128), w=[wt.name])
    return wt


def ffn_block(S, C, Bf, dr, t0, t1, wg, wu, wdn, i_gate, ln_j):
    nt = t1 - t0
    chunks = block_chunks(t0, t1)
    for jg in range(NF // 2):
        wgt = load_w(S, Bf, wg[:, jg * 256:(jg + 1) * 256])
        wut = load_w(S, Bf, wu[:, jg * 256:(jg + 1) * 256])
        for (c0, c1, isc) in chunks:
            tl = list(range(c0 // 128, (c1 + 127) // 128))
            rk = [("uT", i) for i in tl]
            for jj in range(2):
                j = jg * 2 + jj
                pg, pgk = S.ps(); pu, puk = S.ps()
                for k in range(8):
                    mm(S, pg[:, 0:c1 - c0], wgt[:, k, jj * 128:(jj + 1) * 128], Bf.uT[:, k, c0:c1], k == 0, k == 7, rk + [wgt.name], [pgk])
                for k in range(8):
                    mm(S, pu[:, 0:c1 - c0], wut[:, k, jj * 128:(jj + 1) * 128], Bf.uT[:, k, c0:c1], k == 0, k == 7, rk + [wut.name], [puk])
                sg = Bf.sg[Bf.nsg % 2]; Bf.nsg += 1
                S.op('act', lambda E, pg=pg, sg=sg: E.activation(out=sg[:, 0:c1 - c0], in_=pg[:, 0:c1 - c0], func=AF.Silu), r=[pgk], w=[sg.name])
                S.op('dve', lambda E, pu=pu, sg=sg, j=j: E.tensor_tensor(out=Bf.aT[:, j, c0:c1], in0=pu[:, 0:c1 - c0], in1=sg[:, 0:c1 - c0], op=ALU.mult),
                     r=[puk, sg.name], w=[("aT", j, c0)])
    akeys = [("aT", j, c0) for j in range(NF) for (c0, c1, isc) in chunks]
    wdv = wdn.rearrange("(j p) d -> p j d", p=128)
    for hf in range(2):
        S.dma('pool', Bf.wd[:, 0:NF // 2, :], wdv[:, 0:NF // 2, hf * 512:(hf + 1) * 512], w=[("wd", 0)])
        S.dma('pool', Bf.wd[:, NF // 2:NF, :], wdv[:, NF // 2:NF, hf * 512:(hf + 1) * 512], w=[("wd", 1)])
        for i in range(nt):
            isc = 1 if (t0 + i == 0) else 0
            po, pok = S.ps()
            for j in range(NF):
                mm(S, po[:], Bf.aT[:, j, i * 128:(i + 1) * 128], Bf.wd[:, j, :], j == 0, j == NF - 1, akeys + [("wd", 0), ("wd", 1)], [pok])
            mB = C.modB[(i_gate, isc)]
            sl = slice(hf * 512, (hf + 1) * 512)
            S.op('dve', lambda E, po=po, mB=mB, sl=sl: E.tensor_tensor(out=Bf.t1[:, sl], in0=po[:], in1=mB[:, sl], op=ALU.mult),
                 r=[pok, mB.name], w=[("t1", hf)])
            S.op('dve', lambda E, i=i, sl=sl: E.scalar_tensor_tensor(out=Bf.hblk[:, i, sl], in0=Bf.hblk[:, i, sl], scalar=float(DN_ALPHA), in1=Bf.t1[:, sl],
                                                                   op0=ALU.mult, op1=ALU.add), r=[("t1", hf), ("hblk", i)], w=[("hblk", i)])
    post_norm_block(S, C, Bf, nt, ln_j)


def post_norm_block(S, C, Bf, nt, ln_j):
    for i in range(nt):
        ln_stats(S, Bf, Bf.hblk[:, i, :], [("hblk", i)], i)
    ln_finish(S, Bf, nt)
    for i in range(nt):
        S.op('act', lambda E, i=i: E.activation(out=Bf.hblk[:, i, :], in_=Bf.hblk[:, i, :], func=AF.Identity, bias=Bf.nmr[:, i:i + 1], scale=Bf.rstd[:, i:i + 1]),
             r=[("hblk", i), "rstd", "nmr"], w=[("hblk", i)])
        S.op('pool', lambda E, i=i: E.tensor_tensor(out=Bf.hblk[:, i, :], in0=Bf.hblk[:, i, :], in1=C.lnG[ln_j][:], op=ALU.mult),
             r=[("hblk", i), C.lnG[ln_j].name], w=[("hblk", i)])
        S.op('dve', lambda E, i=i: E.tensor_tensor(out=Bf.hblk[:, i, :], in0=Bf.hblk[:, i, :], in1=C.lnB[ln_j][:], op=ALU.add),
             r=[("hblk", i), C.lnB[ln_j].name], w=[("hblk", i)])


def load_block(S, Bf, hsrc, t0, t1):
    for i in range(t1 - t0):
        S.dma('sp', Bf.hblk[:, i, :], hsrc[(t0 + i) * 128:(t0 + i + 1) * 128, :], w=[("hblk", i)])


def modulate_block(S, C, Bf, t0, t1, i_shift, i_scale):
    nt = t1 - t0
    for i in range(nt):
        ln_stats(S, Bf, Bf.hblk[:, i, :], [("hblk", i)], i)
    ln_finish(S, Bf, nt)
    for i in range(nt):
        modulate_T(S, C, Bf, i, 1 if (t0 + i == 0) else 0, i_shift, i_scale)


def dram_in(nc, name, shape, dt=F32):
    return nc.dram_tensor(name, list(shape), dt, kind="ExternalInput").ap()


def dram_out(nc, name, shape, dt=F32):
    return nc.dram_tensor(name, list(shape), dt, kind="ExternalOutput").ap()


def common_drams(nc, ffn=True):
    dr = {}
    dr['c_row'] = dram_in(nc, "c_row", [D]); dr['c_ctx'] = dram_in(nc, "c_ctx", [D])
    dr['w_mod'] = dram_in(nc, "w_mod", [D, 9 * D]); dr['b_mod'] = dram_in(nc, "b_mod", [9 * D])
    dr['ln_g'] = dram_in(nc, "ln_g", [3, D]); dr['ln_b'] = dram_in(nc, "ln_b", [3, D])
    if not ffn:
        return dr
    dr['w_gate'] = dram_in(nc, "w_gate", [D, DFF]); dr['w_up'] = dram_in(nc, "w_up", [D, DFF]); dr['w_down'] = dram_in(nc, "w_down", [DFF, D])
    return dr


def evac(S, n, out, in_, r, w):
    if n % 2 == 0:
        S.op('act', lambda E: E.activation(out=out, in_=in_, func=AF.Identity), r=r, w=w)
    else:
        S.op('dve', lambda E: E.tensor_copy(out=out, in_=in_), r=r, w=w)


def build_phase_a(i_shift=0, i_scale=1, i_gate=2, ln_j=0, with_win=True):
    nc = bass.Bass("TRN2", target_bir_lowering=False)
    dr = common_drams(nc)
    hin = dram_in(nc, "hin", [TTOK, D])
    h1 = dram_out(nc, "h1", [TTOK, D])
    if with_win:
        w_in = dram_in(nc, "w_in", [D, 5120])
        w_kp = dram_in(nc, "w_kp", [D, 512]); w_qp = dram_in(nc, "w_qp", [D, 512])
        ropeC = dram_in(nc, "ropeC", [128, 4096]); ropeS = dram_in(nc, "ropeS", [128, 4096])
        usT = dram_out(nc, "usT", [256, TTOK], BF16)
        kT = dram_out(nc, "kT", [512, TTOK], BF16); qT = dram_out(nc, "qT", [512, TTOK], BF16)
        vv = dram_out(nc, "v", [TTOK, 512], BF16); zf = dram_out(nc, "zf", [TTOK, 256], BF16)
    with ExitStack() as es:
        S = Sched(nc, es)
        Bf = alloc_tok_bufs(S)
        C = setup_common(S, nc, dr, ([i_shift, i_scale] + ([3, 4] if with_win else []), [i_gate], 0.5), [ln_j], Bf)
        if with_win:
            rC = S.sb("rC", [128, 1024], F32); rS = S.sb("rS", [128, 1024], F32)
            zst = [S.sb("zst%d" % i, [128, 512], BF16) for i in range(4)]
            rt = [S.sb("rt%d" % i, [128, 512], F32) for i in range(2)]
        nz = [0]
        outkeys = []

        def stage():
            t = zst[nz[0] % 4]; nz[0] += 1
            return t

        for (t0, t1) in BLOCKS:
            nt = t1 - t0
            chunks = block_chunks(t0, t1)
            lat0 = (max(t0, 1) - 1) * 128
            nlat = (t1 - max(t0, 1)) * 128
            load_block(S, Bf, hin, t0, t1)
            modulate_block(S, C, Bf, t0, t1, i_shift, i_scale)
            ffn_block(S, C, Bf, dr, t0, t1, dr['w_gate'], dr['w_up'], dr['w_down'], i_gate, ln_j)
            for i in range(nt):
                S.dma('sp', h1[(t0 + i) * 128:(t0 + i + 1) * 128, :], Bf.hblk[:, i, :], r=[("hblk", i)], w=[("o_h1", t0 + i)])
                outkeys.append(("o_h1", t0 + i))
            if not with_win:
                continue
            modulate_block(S, C, Bf, t0, t1, 3, 4)
            S.dma('sp', rC[:, 0:nlat], ropeC[:, lat0:lat0 + nlat], w=["rC"])
            S.dma('sp', rS[:, 0:nlat], ropeS[:, lat0:lat0 + nlat], w=["rS"])
            ukeys = lambda c0, c1: [("uT", i) for i in range(c0 // 128, (c1 + 127) // 128)]
            wt = load_w(S, Bf, w_in[:, 0:256])
            for (c0, c1, isc) in chunks:
                for mi in range(2):
                    pz, pzk = S.ps()
                    for k in range(8):
                        mm(S, pz[:, 0:c1 - c0], wt[:, k, mi * 128:(mi + 1) * 128], Bf.uT[:, k, c0:c1], k == 0, k == 7, ukeys(c0, c1) + [wt.name], [pzk])
                    st = stage()
                    evac(S, nz[0], st[:, 0:c1 - c0], pz[:, 0:c1 - c0], [pzk], [st.name])
                    ok = ("o_us", t0, c0, mi); outkeys.append(ok)
                    S.dma('sp', usT[mi * 128:(mi + 1) * 128, t0 * 128 + c0:t0 * 128 + c1], st[:, 0:c1 - c0], r=[st.name], w=[ok])
            for (col, wp, dst, nm) in ((COL_K, w_kp, kT, "k"), (COL_Q, w_qp, qT, "q")):
                for hp in range(4):
                    wt = Bf.wb[Bf.nwb % 4]; Bf.nwb += 1
                    S.dma('pool', wt[:, :, 0:128], w_in[:, col + hp * 128:col + (hp + 1) * 128].rearrange("(k p) c -> p k c", p=128), w=[wt.name])
                    S.dma('pool', wt[:, :, 128:256], wp[:, hp * 128:(hp + 1) * 128].rearrange("(k p) c -> p k c", p=128), w=[wt.name])
                    for (c0, c1, isc) in chunks:
                        n = c1 - c0
                        pz, pzk = S.ps()
                        for k in range(8):
                            mm(S, pz[:, 0:n], wt[:, k, 0:128], Bf.uT[:, k, c0:c1], k == 0, k == 7, ukeys(c0, c1) + [wt.name], [pzk])
                        st = stage()
                        if isc:
                            evac(S, nz[0], st[:, 0:n], pz[:, 0:n], [pzk], [st.name])
                        else:
                            pp, ppk = S.ps()
                            for k in range(8):
                                mm(S, pp[:, 0:n], wt[:, k, 128:256], Bf.uT[:, k, c0:c1], k == 0, k == 7, ukeys(c0, c1) + [wt.name], [ppk])
                            p0 = c0 - (128 if t0 == 0 else 0)
                            S.op('dve', lambda E, pz=pz, n=n, p0=p0: E.tensor_tensor(out=rt[0][:, 0:n], in0=pz[:, 0:n], in1=rC[:, p0:p0 + n], op=ALU.mult),
                                 r=[pzk, "rC"], w=["rt0"])
                            S.op('dve', lambda E, pp=pp, n=n, p0=p0: E.tensor_tensor(out=rt[1][:, 0:n], in0=pp[:, 0:n], in1=rS[:, p0:p0 + n], op=ALU.mult),
                                 r=[ppk, "rS"], w=["rt1"])
                            S.op('pool', lambda E, st=st, n=n: E.tensor_tensor(out=st[:, 0:n], in0=rt[0][:, 0:n], in1=rt[1][:, 0:n], op=ALU.add),
                                 r=["rt0", "rt1"], w=[st.name])
                        ok = ("o_" + nm, t0, c0, hp); outkeys.append(ok)
                        S.dma('sp', dst[hp * 128:(hp + 1) * 128, t0 * 128 + c0:t0 * 128 + c1], st[:, 0:n], r=[st.name], w=[ok])
            w0 = load_w(S, Bf, w_in[:, COL_V:COL_V + 256]); w1 = load_w(S, Bf, w_in[:, COL_V + 256:COL_V + 512])
            for i in range(nt):
                pz, pzk = S.ps()
                for hh, wt in ((0, w0), (1, w1)):
                    for k in range(8):
                        mm(S, pz[:, hh * 256:(hh + 1) * 256], Bf.uT[:, k, i * 128:(i + 1) * 128], wt[:, k, :], k == 0, k == 7, [("uT", i), wt.name], [pzk])
                st = stage()
                evac(S, nz[0], st[:], pz[:], [pzk], [st.name])
                ok = ("o_v", t0 + i); outkeys.append(ok)
                S.dma('sp', vv[(t0 + i) * 128:(t0 + i + 1) * 128, :], st[:], r=[st.name], w=[ok])
            wt = load_w(S, Bf, w_in[:, COL_FNET:COL_FNET + 256])
            for i in range(nt):
                pz, pzk = S.ps()
                for k in range(8):
                    mm(S, pz[:, 0:256], Bf.uT[:, k, i * 128:(i + 1) * 128], wt[:, k, :], k == 0, k == 7, [("uT", i), wt.name], [pzk])
                st = stage()
                evac(S, nz[0], st[:, 0:256], pz[:, 0:256], [pzk], [st.name])
                ok = ("o_zf", t0 + i); outkeys.append(ok)
                S.dma('sp', zf[(t0 + i) * 128:(t0 + i + 1) * 128, :], st[:, 0:256], r=[st.name], w=[ok])
        S.finish(outkeys)
    return nc


def rope_tables():
    nf = 16
    t = np.arange(L, dtype=np.int32)
    pos = np.stack([t // GRID_W, t % GRID_W], axis=-1).astype(np.float32)
    inv_freq = (np.float32(10000.0) ** (-np.arange(nf, dtype=np.float32) / np.float32(nf))).astype(np.float32)
    ang = (pos[:, :, None] * inv_freq).astype(np.float32)
    cos = np.cos(ang).astype(np.float32); sin = np.sin(ang).astype(np.float32)
    Ct = np.zeros((64, L), np.float32); St = np.zeros((64, L), np.float32); perm = np.zeros(64, np.int64)
    for a in range(2):
        for half in range(2):
            for j in range(nf):
                i = a * 32 + half * 16 + j
                Ct[i] = cos[:, a, j]
                St[i] = -sin[:, a, j] if half == 0 else sin[:, a, j]
                perm[i] = i + 16 if half == 0 else i - 16
    return np.concatenate([Ct, Ct], 0), np.concatenate([St, St], 0), perm


def core_bh(core):
    return core // 2, core % 2


def phase_a_inputs(inp, l, s, hfull, hcfull):
    Ct, St, perm = rope_tables()
    permcols = np.concatenate([h * 64 + perm for h in range(8)])
    maps = []
    for core in range(8):
        b, hf = core_bh(core)
        m = {}
        m['hin'] = np.ascontiguousarray(np.concatenate([hcfull[b, hf * 128:(hf + 1) * 128], hfull[b, hf * 4096:(hf + 1) * 4096]], 0))
        m['c_row'] = np.ascontiguousarray(inp['c'][b]); m['c_ctx'] = np.ascontiguousarray(inp['c_ctx'])
        m['w_mod'] = np.ascontiguousarray(inp['w_mod'][l]); m['b_mod'] = np.ascontiguousarray(inp['b_mod'][l])
        m['ln_g'] = np.ascontiguousarray(inp['ln_g'][l]); m['ln_b'] = np.ascontiguousarray(inp['ln_b'][l])
        m['w_gate'] = np.ascontiguousarray(inp['ffn_w_gate'][l, s]); m['w_up'] = np.ascontiguousarray(inp['ffn_w_up'][l, s])
        m['w_down'] = np.ascontiguousarray(inp['ffn_w_down'][l, s])
        m['w_in'] = np.ascontiguousarray(inp['w_in'][l])
        m['w_kp'] = np.ascontiguousarray(inp['w_in'][l][:, COL_K + permcols]); m['w_qp'] = np.ascontiguousarray(inp['w_in'][l][:, COL_Q + permcols])
        m['ropeC'] = np.ascontiguousarray(Ct[:, hf * 4096:(hf + 1) * 4096]); m['ropeS'] = np.ascontiguousarray(St[:, hf * 4096:(hf + 1) * 4096])
        maps.append(m)
    return maps


def fnet_tables():
    ta = np.arange(128); ka = np.arange(128)
    a = 2 * np.pi * ((ta[:, None] * ka[None, :]) % 128) / 128.0
    F128 = np.concatenate([np.cos(a), -np.sin(a)], 1)
    tb = np.arange(64)
    w = 2 * np.pi * ((tb[:, None] * ka[None, :]) % 8192) / 8192.0
    Wc = np.tile(np.cos(w), (2, 1)); Ws = np.tile(np.sin(w), (2, 1))
    kb = np.arange(64)
    b = 2 * np.pi * ((tb[:, None] * kb[None, :]) % 64) / 64.0
    Cb = np.zeros((128, 128)); Sb = np.zeros((128, 128))
    for c2 in range(2):
        Cb[c2 * 64:(c2 + 1) * 64, c2 * 64:(c2 + 1) * 64] = np.cos(b); Sb[c2 * 64:(c2 + 1) * 64, c2 * 64:(c2 + 1) * 64] = np.sin(b)
    G1 = np.concatenate([Cb, -Sb], 1)
    G2 = np.concatenate([Sb, Cb], 1)
    t = np.arange(256)
    c256 = 2 * np.pi * ((t[:, None] * t[None, :]) % 256) / 256.0
    F256 = np.concatenate([np.cos(c256), -np.sin(c256)], 1)
    f = lambda x: np.ascontiguousarray(x.astype(np.float32))
    return dict(F128=f(F128), Wc=f(Wc), Ws=f(Ws), G1=f(G1), G2=f(G2), F256=f(F256))


def build_fnet(with_ctx):
    nc = bass.Bass("TRN2", target_bir_lowering=False)
    zf = dram_in(nc, "zf", [L, 128], BF16)
    tabs = {k: dram_in(nc, "t_" + k, shp) for k, shp in (("F128", [128, 256]), ("Wc", [128, 128]), ("Ws", [128, 128]),
                                                        ("G1", [128, 256]), ("G2", [128, 256]), ("F256", [256, 512]))}
    yfr = dram_out(nc, "yfr", [L, 128], BF16); yfi = dram_out(nc, "yfi", [L, 128], BF16)
    if with_ctx:
        zfc = dram_in(nc, "zfc", [LC, 128], BF16)
        ycr = dram_out(nc, "ycr", [LC, 128], BF16); yci = dram_out(nc, "yci", [LC, 128], BF16)
    with ExitStack() as es:
        S = Sched(nc, es)
        zt0 = S.sb("zt0", [128, 64, 128], BF16)
        S.dma('sp', zt0[:], zf.rearrange("(ta tb) c -> ta tb c", tb=64), w=["zt0"])
        zt = S.sb("zt", [128, 128, 64], BF16)
        for qq, eng in enumerate(['dve', 'pool', 'act', 'dve']):
            src = zt0[:, :, qq * 32:(qq + 1) * 32].rearrange("p t c -> p c t")
            if eng == 'act':
                S.op(eng, lambda E, src=src, qq=qq: E.activation(out=zt[:, qq * 32:(qq + 1) * 32, :], in_=src, func=AF.Identity), r=["zt0"], w=["zt"])
            else:
                S.op(eng, lambda E, src=src, qq=qq: E.tensor_copy(out=zt[:, qq * 32:(qq + 1) * 32, :], in_=src), r=["zt0"], w=["zt"])
        T = {}
        for k in ("F128", "G1", "G2"):
            T[k] = S.sb("T" + k, [128, 256], BF16)
            S.dma('pool', T[k][:], tabs[k], w=["T" + k])
        for k in ("Wc", "Ws"):
            T[k] = S.sb("T" + k, [128, 128], F32)
            S.dma('sp', T[k][:], tabs[k], w=["T" + k])
        X2r = S.sb("X2r", [128, 64, 128], BF16); X2i = S.sb("X2i", [128, 64, 128], BF16)
        Yr = S.sb("Yr", [128, 64, 128], BF16); Yi = S.sb("Yi", [128, 64, 128], BF16)
        tmp = [S.sb("ft%d" % i, [128, 2, 128], F32) for i in range(4)]
        for cq in range(32):
            ps, pk = S.ps()
            for u in range(2):
                cp = cq * 2 + u
                mm(S, ps[:, u * 256:(u + 1) * 256], zt[:, 2 * cp:2 * cp + 2, :], T["F128"][:], True, True, ["zt", "TF128"], [pk])
            pv = ps[:].rearrange("p (u r k) -> p u r k", u=2, r=2)
            xr = pv[:, :, 0, :]; xi = pv[:, :, 1, :]
            wc = T["Wc"][:, None, :].to_broadcast([128, 2, 128]); ws = T["Ws"][:, None, :].to_broadcast([128, 2, 128])
            S.op('dve', lambda E: E.tensor_tensor(out=tmp[0][:], in0=xr, in1=wc, op=ALU.mult), r=[pk, "TWc"], w=["ft0"])
            S.op('dve', lambda E: E.tensor_tensor(out=tmp[1][:], in0=xi, in1=ws, op=ALU.mult), r=[pk, "TWs"], w=["ft1"])
            S.op('dve', lambda E: E.tensor_tensor(out=tmp[2][:], in0=xi, in1=wc, op=ALU.mult), r=[pk, "TWc"], w=["ft2"])
            S.op('dve', lambda E: E.tensor_tensor(out=tmp[3][:], in0=xr, in1=ws, op=ALU.mult), r=[pk, "TWs"], w=["ft3"])
            S.op('pool', lambda E, cq=cq: E.tensor_tensor(out=X2r[:, 2 * cq:2 * cq + 2, :], in0=tmp[0][:], in1=tmp[1][:], op=ALU.add),
                 r=["ft0", "ft1"], w=[("X2r", cq)])
            S.op('pool', lambda E, cq=cq: E.tensor_tensor(out=X2i[:, 2 * cq:2 * cq + 2, :], in0=tmp[2][:], in1=tmp[3][:], op=ALU.subtract),
                 r=["ft2", "ft3"], w=[("X2i", cq)])
        for cp in range(64):
            ps, pk = S.ps()
            mm(S, ps[:, 0:256], X2r[:, cp, :], T["G1"][:], True, False, [("X2r", cp // 2), "TG1"], [pk])
            mm(S, ps[:, 0:256], X2i[:, cp, :], T["G2"][:], False, True, [("X2i", cp // 2), "TG2"], [pk])
            pv = ps[:, 0:256].rearrange("p (r c k) -> p r c k", r=2, c=2)
            S.op('act', lambda E, cp=cp, pv=pv: E.activation(out=Yr[:, :, 2 * cp:2 * cp + 2].rearrange("p k c -> p c k"), in_=pv[:, 0, :, :], func=AF.Identity), r=[pk], w=[("Yr", cp)])
            S.op('dve', lambda E, cp=cp, pv=pv: E.tensor_copy(out=Yi[:, :, 2 * cp:2 * cp + 2].rearrange("p k c -> p c k"), in_=pv[:, 1, :, :]), r=[pk], w=[("Yi", cp)])
        S.dma('sp', yfr.rearrange("(kb ka) c -> ka kb c", ka=128), Yr[:], r=[("Yr", cp) for cp in range(64)], w=["o_r"])
        S.dma('sp', yfi.rearrange("(kb ka) c -> ka kb c", ka=128), Yi[:], r=[("Yi", cp) for cp in range(64)], w=["o_i"])
        outk = ["o_r", "o_i"]
        if with_ctx:
            zc = S.sb("zc", [128, 2, 128], BF16)
            S.dma('sp', zc[:], zfc.rearrange("(a p) c -> p a c", p=128), w=["zc"])
            F2 = S.sb("F2", [128, 2, 512], BF16)
            S.dma('pool', F2[:], tabs["F256"].rearrange("(a p) k -> p a k", p=128), w=["F2"])
            yc = S.sb("yc", [128, 2, 2, 128], BF16)
            for kc in range(2):
                for ri in range(2):
                    ps, pk = S.ps()
                    for a in range(2):
                        mm(S, ps[:, 0:128], F2[:, a, ri * 256 + kc * 128:ri * 256 + (kc + 1) * 128], zc[:, a, :], a == 0, a == 1, ["zc", "F2"], [pk])
                    S.op('act', lambda E, ps=ps, kc=kc, ri=ri: E.activation(out=yc[:, kc, ri, :], in_=ps[:, 0:128], func=AF.Identity, scale=float(np.sqrt(32.0))), r=[pk], w=["yc"])
            S.dma('sp', ycr.rearrange("(a p) c -> p a c", p=128), yc[:, :, 0, :], r=["yc"], w=["o_cr"])
            S.dma('sp', yci.rearrange("(a p) c -> p a c", p=128), yc[:, :, 1, :], r=["yc"], w=["o_ci"])
            outk += ["o_cr", "o_ci"]
        S.finish(outk)
    return nc


NA_VARIANT_JP = [10, 0, 1, 62, 63]


def na_variant(jp):
    return {0: 1, 1: 2, 62: 3, 63: 4}.get(jp, 0)


def na_bias_tables(rpb4):
    out = np.full((5, 4, 5, 2, 64, 2, 64), -30000.0, np.float32)
    c = np.arange(64); kc = np.arange(64)
    cs = np.clip(c - 8, 0, 48)
    validc = (kc[:, None] >= cs[None, :]) & (kc[:, None] < cs[None, :] + 16)
    dc = np.clip(kc[:, None] - c[None, :] + 15, 0, 30)
    for vi, jp in enumerate(NA_VARIANT_JP):
        kp0 = int(np.clip(jp - 2, 0, 59))
        for i in range(5):
            for krl in range(2):
                kr = 2 * (kp0 + i) + krl
                for rl in range(2):
                    r = 2 * jp + rl
                    r0 = int(np.clip(r - 4, 0, 120))
                    if not (r0 <= kr < r0 + 8):
                        continue
                    dr = kr - r + 7
                    g = rpb4[:, dr, :][:, dc]
                    out[vi, :, i, krl, :, rl, :] = np.where(validc[None], g, np.float32(-30000.0))
    return np.ascontiguousarray(out.reshape(5, 4, 5, 128, 128))


def build_na(with_ctx):
    nc = bass.Bass("TRN2", target_bir_lowering=False)
    qT = dram_in(nc, "qT", [256, L], BF16); kT = dram_in(nc, "kT", [256, L], BF16); v = dram_in(nc, "v", [L, 256], BF16)
    kcT = dram_in(nc, "kcT", [256, LC], BF16); vc = dram_in(nc, "vc", [LC, 256], BF16)
    bias = dram_in(nc, "bias", [5, 4, 5, 128, 128])
    ynT = dram_out(nc, "ynT", [256, L], BF16)
    if with_ctx:
        qcT = dram_in(nc, "qcT", [256, LC], BF16)
        yncT = dram_out(nc, "yncT", [256, LC], BF16)
    with ExitStack() as es:
        S = Sched(nc, es)
        qt = S.sb("qt", [128, 2, L], BF16); kt = S.sb("kt", [128, 2, L], BF16)
        vt = S.sb("vt", [128, 64, 256], BF16)
        kct = S.sb("kct", [128, 2, LC], BF16); vct = S.sb("vct", [128, 2, 256], BF16)
        bt = S.sb("bt", [128, 100, 128], BF16)
        ones = S.sb("ones", [128, 64], BF16)
        S.op('pool', lambda E: E.memset(ones[:], 1.0), w=["ones"])
        for p in range(2):
            S.dma('sp', qt[:, p, :], qT[p * 128:(p + 1) * 128, :], w=[("qt", p)])
            S.dma('sp', kt[:, p, :], kT[p * 128:(p + 1) * 128, :], w=[("kt", p)])
        S.dma('sp', vt[:], v.rearrange("(kp p) c -> p kp c", p=128), w=["vt"])
        S.dma('sp', kct[:], kcT.rearrange("(a p) t -> p a t", p=128), w=["kct"])
        S.dma('sp', vct[:], vc.rearrange("(a p) c -> p a c", p=128), w=["vct"])
        S.dma('pool', bt[:], bias.rearrange("v h i k q -> k (v h i) q"), w=["bt"])
        tt = [S.sb("tt%d" % i, [128, 640], F32) for i in range(2)]
        PT = [S.sb("PT%d" % i, [128, 896], BF16) for i in range(2)]
        rd = [S.sb("rd%d" % i, [64, 128], F32) for i in range(2)]
        yst = [S.sb("yst%d" % i, [64, 1024], BF16) for i in range(2)]
        outk = []
        it = 0
        for h in range(4):
            p = h // 2; hh = h % 2; ps_ = slice(hh * 64, (hh + 1) * 64)
            for jp in range(64):
                kp0 = int(np.clip(jp - 2, 0, 59)); var = na_variant(jp)
                q_ap = qt[ps_, p, jp * 128:(jp + 1) * 128]
                pA, pAk = S.ps(); pB, pBk = S.ps()
                for i in range(5):
                    dst = pA[:, i * 128:(i + 1) * 128] if i < 4 else pB[:, 0:128]
                    mm(S, dst, kt[ps_, p, (kp0 + i) * 128:(kp0 + i + 1) * 128], q_ap, True, True, [("kt", p), ("qt", p)], [pAk if i < 4 else pBk])
                for a in range(2):
                    mm(S, pB[:, 128 + a * 128:256 + a * 128], kct[ps_, p, a * 128:(a + 1) * 128], q_ap, True, True, ["kct", ("qt", p)], [pBk])
                t = tt[it % 2]; P = PT[it % 2]; r_ = rd[it % 2]; tk = t.name; Pk = P.name
                b0 = (var * 4 + h) * 5
                S.op('dve', lambda E, pA=pA, t=t, b0=b0: E.scalar_tensor_tensor(out=t[:, 0:512], in0=pA[:], scalar=0.125,
                                                                                in1=bt[:, b0:b0 + 4, :].rearrange("k i q -> k (i q)"), op0=ALU.mult, op1=ALU.add),
                     r=[pAk, "bt"], w=[(tk, 0)])
                S.op('dve', lambda E, pB=pB, t=t, b0=b0: E.scalar_tensor_tensor(out=t[:, 512:640], in0=pB[:, 0:128], scalar=0.125,
                                                                                in1=bt[:, b0 + 4, :], op0=ALU.mult, op1=ALU.add),
                     r=[pBk, "bt"], w=[(tk, 1)])
                S.op('act', lambda E, t=t, P=P: E.activation(out=P[:, 0:640], in_=t[:], func=AF.Exp), r=[(tk, 0), (tk, 1)], w=[(Pk, 0)])
                S.op('act', lambda E, pB=pB, P=P: E.activation(out=P[:, 640:896], in_=pB[:, 128:384], func=AF.Exp, scale=0.125), r=[pBk], w=[(Pk, 1)])
                pO, pOk = S.ps()
                for g in range(2):
                    for i in range(7):
                        if g == 0:
                            lt = vt[:, kp0 + i, h * 64:(h + 1) * 64] if i < 5 else vct[:, i - 5, h * 64:(h + 1) * 64]
                            lk = "vt" if i < 5 else "vct"
                        else:
                            lt = ones[:]; lk = "ones"
                        mm(S, pO[0:64, g * 128:(g + 1) * 128], lt, P[:, i * 128:(i + 1) * 128], i == 0, i == 6, [lk, (Pk, 0), (Pk, 1)], [pOk])
                ys = yst[(jp // 8 + h * 8) % 2]; yk = ys.name
                S.op('dve', lambda E, pO=pO, r_=r_: E.reciprocal(out=r_[:], in_=pO[0:64, 128:256]), r=[pOk], w=[r_.name])
                S.op('dve', lambda E, pO=pO, r_=r_, ys=ys, jp=jp: E.tensor_tensor(out=ys[:, (jp % 8) * 128:(jp % 8 + 1) * 128], in0=pO[0:64, 0:128], in1=r_[:], op=ALU.mult),
                     r=[pOk, r_.name], w=[yk])
                it += 1
                if jp % 8 == 7:
                    ok = ("o", h, jp); outk.append(ok)
                    S.dma('sp', ynT[h * 64:(h + 1) * 64, (jp - 7) * 128:(jp + 1) * 128], ys[:], r=[yk], w=[ok])
        if with_ctx:
            qct = S.sb("qct", [128, 2, LC], BF16)
            S.dma('sp', qct[:], qcT.rearrange("(a p) t -> p a t", p=128), w=["qct"])
            Pc = S.sb("Pc", [128, 512], BF16)
            rdc = S.sb("rdc", [64, 256], F32); yc = S.sb("yc", [64, 4, 256], BF16)
            for h in range(4):
                p = h // 2; hh = h % 2; ps_ = slice(hh * 64, (hh + 1) * 64)
                pA, pAk = S.ps()
                for a in range(2):
                    mm(S, pA[:, a * 256:(a + 1) * 256], kct[ps_, p, a * 128:(a + 1) * 128], qct[ps_, p, :], True, True, ["kct", "qct"], [pAk])
                S.op('act', lambda E, pA=pA: E.activation(out=Pc[:], in_=pA[:], func=AF.Exp, scale=0.125), r=[pAk], w=["Pc"])
                pO, pOk = S.ps()
                for g in range(2):
                    for a in range(2):
                        lt = vct[:, a, h * 64:(h + 1) * 64] if g == 0 else ones[:]
                        mm(S, pO[0:64, g * 256:(g + 1) * 256], lt, Pc[:, a * 256:(a + 1) * 256], a == 0, a == 1, ["vct", "ones", "Pc"], [pOk])
                S.op('dve', lambda E, pO=pO: E.reciprocal(out=rdc[:], in_=pO[0:64, 256:512]), r=[pOk], w=["rdc"])
                S.op('dve', lambda E, pO=pO, h=h: E.tensor_tensor(out=yc[:, h, :], in0=pO[0:64, 0:256], in1=rdc[:], op=ALU.mult), r=[pOk, "rdc"], w=["yc"])
            S.dma('sp', yncT.rearrange("(h d) t -> d h t", d=64), yc[:], r=["yc"], w=["o_c"])
            outk.append("o_c")
        S.finish(outk)
    return nc


LT = L + LC
SEG = 2112
MAGIC = 12582912.0
TWO_PI_LO = 6.283185


def s5_host_params(inp, l, half):
    gs = slice(half * 8, half * 8 + 8)
    def sc(a):
        a = a[:, gs]
        return np.ascontiguousarray(a.reshape(2, 4, 2, 64).transpose(2, 3, 0, 1).reshape(128, 8))
    ldt = np.repeat(inp['ssm_log_dt'][l][:, :, None], 64, axis=2)
    P = {'ldt': sc(ldt), 'are': sc(inp['ssm_a_re'][l]), 'aim': sc(inp['ssm_a_im'][l])}
    for nm, key in (('bre', 'ssm_b_re'), ('bim', 'ssm_b_im')):
        b = inp[key][l][:, gs]
        E = np.zeros((128, 8, 128), np.float32)
        for d in range(2):
            for gp in range(4):
                for j in range(2):
                    g = 2 * gp + j
                    E[j * 64:(j + 1) * 64, d * 4 + gp, g * 16:(g + 1) * 16] = b[d, g]
        P[nm] = E
    for nm, key in (('cre', 'ssm_c_re'), ('cim', 'ssm_c_im')):
        c = inp[key][l][:, gs]
        E = np.zeros((128, 8, 32), np.float32)
        for d in range(2):
            for gp in range(4):
                for j in range(2):
                    g = 2 * gp + j
                    E[j * 64:(j + 1) * 64, d * 4 + gp, j * 16:(j + 1) * 16] = c[d, g].T
        P[nm] = E
    P['tpos'] = np.arange(LT, dtype=np.float32)
    return P


def build_s5():
    nc = bass.Bass("TRN2", target_bir_lowering=False)
    usf = dram_in(nc, "usf", [128, LT], BF16); usb = dram_in(nc, "usb", [128, LT], BF16)
    ldt = dram_in(nc, "ldt", [128, 8]); are = dram_in(nc, "are", [128, 8]); aim = dram_in(nc, "aim", [128, 8])
    bre = dram_in(nc, "bre", [128, 8, 128]); bim = dram_in(nc, "bim", [128, 8, 128])
    cre = dram_in(nc, "cre", [128, 8, 32]); cim = dram_in(nc, "cim", [128, 8, 32])
    tpos = dram_in(nc, "tpos", [LT])
    ysf = dram_out(nc, "ysf", [128, LT]); ysb = dram_out(nc, "ysb", [128, LT])
    with ExitStack() as es:
        S = Sched(nc, es)
        sb = S.sb
        us = [sb("usf_t", [128, LT], BF16), sb("usb_t", [128, LT], BF16)]
        S.dma('sp', us[0][:], usf, w=["us0"]); S.dma('sp', us[1][:], usb, w=["us1"])
        tI = sb("tI", [128, LT], F32)
        S.dma('sp', tI[:], tpos.partition_broadcast(128), w=["tI"])
        pr = {}
        for nm, src, shp in (("ldt", ldt, [128, 8]), ("are", are, [128, 8]), ("aim", aim, [128, 8]), ("bre", bre, [128, 8, 128]),
                             ("bim", bim, [128, 8, 128]), ("cre", cre, [128, 8, 32]), ("cim", cim, [128, 8, 32])):
            pr[nm] = sb("p_" + nm, shp, F32)
            S.dma('sp', pr[nm][:], src, w=[nm])
        sm = {nm: sb("s_" + nm, [128, 8], F32) for nm in ("dt", "adt", "mag", "f", "t1", "t2", "sin", "cos", "abr", "abi", "den", "nr", "fr", "fi", "nfi", "u1", "u2")}

        def dv(out, fn, r, w):
            S.op('dve', fn, r=r, w=w)
        TT = lambda o, a, b, op: (lambda E: E.tensor_tensor(out=sm[o][:], in0=sm[a][:] if a in sm else pr[a][:], in1=sm[b][:] if b in sm else pr[b][:], op=op))
        TS = lambda o, a, s1, s2, op0, op1=ALU.bypass: (lambda E: E.tensor_scalar(out=sm[o][:], in0=sm[a][:] if a in sm else pr[a][:], scalar1=s1, scalar2=s2, op0=op0, op1=op1))
        S.op('act', lambda E: E.activation(out=sm["dt"][:], in_=pr["ldt"][:], func=AF.Exp), r=["ldt"], w=["dt"])
        dv("adt", TT("adt", "are", "dt", ALU.mult), ["are", "dt"], ["adt"])
        S.op('act', lambda E: E.activation(out=sm["mag"][:], in_=sm["adt"][:], func=AF.Exp), r=["adt"], w=["mag"])
        dv("f", TT("f", "aim", "dt", ALU.mult), ["aim", "dt"], ["f"])
        dv("f", TS("f", "f", float(1.0 / (2 * np.pi)), None, ALU.mult), ["f"], ["f"])
        dv("t1", TS("t1", "f", MAGIC, MAGIC, ALU.add, ALU.subtract), ["f"], ["t1"])
        dv("t1", TT("t1", "f", "t1", ALU.subtract), ["f", "t1"], ["t1"])
        S.op('act', lambda E: E.activation(out=sm["sin"][:], in_=sm["t1"][:], func=AF.Sin, scale=TWO_PI_LO), r=["t1"], w=["sin"])
        dv("t2", TS("t2", "f", 0.25, None, ALU.add), ["f"], ["t2"])
        dv("u1", TS("u1", "t2", MAGIC, MAGIC, ALU.add, ALU.subtract), ["t2"], ["u1"])
        dv("t2", TT("t2", "t2", "u1", ALU.subtract), ["t2", "u1"], ["t2"])
        S.op('act', lambda E: E.activation(out=sm["cos"][:], in_=sm["t2"][:], func=AF.Sin, scale=TWO_PI_LO), r=["t2"], w=["cos"])
        dv("abr", TT("abr", "mag", "cos", ALU.mult), ["mag", "cos"], ["abr"])
        dv("abi", TT("abi", "mag", "sin", ALU.mult), ["mag", "sin"], ["abi"])
        dv("den", TT("den", "are", "are", ALU.mult), ["are"], ["den"])
        dv("u1", TT("u1", "aim", "aim", ALU.mult), ["aim"], ["u1"])
        dv("den", TT("den", "den", "u1", ALU.add), ["den", "u1"], ["den"])
        S.op('dve', lambda E: E.reciprocal(out=sm["den"][:], in_=sm["den"][:]), r=["den"], w=["den"])
        dv("nr", TS("nr", "abr", -1.0, None, ALU.add), ["abr"], ["nr"])
        dv("u1", TT("u1", "nr", "are", ALU.mult), ["nr", "are"], ["u1"])
        dv("u2", TT("u2", "abi", "aim", ALU.mult), ["abi", "aim"], ["u2"])
        dv("fr", TT("fr", "u1", "u2", ALU.add), ["u1", "u2"], ["fr"])
        dv("fr", TT("fr", "fr", "den", ALU.mult), ["fr", "den"], ["fr"])
        dv("u1", TT("u1", "abi", "are", ALU.mult), ["abi", "are"], ["u1"])
        dv("u2", TT("u2", "nr", "aim", ALU.mult), ["nr", "aim"], ["u2"])
        dv("fi", TT("fi", "u1", "u2", ALU.subtract), ["u1", "u2"], ["fi"])
        dv("fi", TT("fi", "fi", "den", ALU.mult), ["fi", "den"], ["fi"])
        dv("nfi", TS("nfi", "fi", -1.0, None, ALU.mult), ["fi"], ["nfi"])
        ident_f = sb("ident_f", [128, 128], F32); ident = sb("ident", [128, 128], BF16)
        S.op('pool', lambda E: E.memset(ident_f[:], 1.0), w=["ident_f"])
        S.op('pool', lambda E: E.affine_select(out=ident_f[:], in_=ident_f[:], pattern=[[-1, 128]], compare_op=ALU.is_equal,
                                               fill=0.0, base=0, channel_multiplier=1), r=["ident_f"], w=["ident_f"])
        S.op('dve', lambda E: E.tensor_copy(out=ident[:], in_=ident_f[:]), r=["ident_f"], w=["ident"])
        bt1 = sb("bt1", [128, 128], F32)
        bbE = sb("bbE", [128, 8, 2, 128], BF16)
        BbT = sb("BbT", [128, 8, 2, 128], BF16)
        for pi in range(8):
            S.op('dve', lambda E, pi=pi: E.tensor_scalar(out=bt1[:], in0=pr["bre"][:, pi, :], scalar1=sm["fr"][:, pi:pi + 1], scalar2=None, op0=ALU.mult),
                 r=["bre", "fr"], w=["bt1"])
            S.op('dve', lambda E, pi=pi: E.scalar_tensor_tensor(out=bbE[:, pi, 0, :], in0=pr["bim"][:, pi, :], scalar=sm["nfi"][:, pi:pi + 1], in1=bt1[:],
                                                                op0=ALU.mult, op1=ALU.add), r=["bim", "nfi", "bt1"], w=[("bbE", pi)])
            S.op('dve', lambda E, pi=pi: E.tensor_scalar(out=bt1[:], in0=pr["bim"][:, pi, :], scalar1=sm["fr"][:, pi:pi + 1], scalar2=None, op0=ALU.mult),
                 r=["bim", "fr", ("bbE", pi)], w=["bt1"])
            S.op('dve', lambda E, pi=pi: E.scalar_tensor_tensor(out=bbE[:, pi, 1, :], in0=pr["bre"][:, pi, :], scalar=sm["fi"][:, pi:pi + 1], in1=bt1[:],
                                                                op0=ALU.mult, op1=ALU.add), r=["bre", "fi", "bt1"], w=[("bbE", pi)])
            pt, pk = S.ps(); ptb = pt.bitcast(BF16)
            for ri in range(2):
                S.op('pe', lambda E, pi=pi, ri=ri: E.transpose(out=ptb[:, ri * 128:(ri + 1) * 128], in_=bbE[:, pi, ri, :], identity=ident[:]),
                     r=[("bbE", pi), "ident"], w=[pk])
            S.op('act', lambda E, pi=pi, ptb=ptb: E.activation(out=BbT[:, pi, :, :], in_=ptb[:, 0:256].rearrange("p (r c) -> p r c", r=2), func=AF.Identity),
                 r=[pk], w=[("BbT", pi)])
        CTr = sb("CTr", [128, 8, 32], BF16); CTi = sb("CTi", [128, 8, 32], BF16)
        S.op('dve', lambda E: E.tensor_copy(out=CTr[:], in_=pr["cre"][:]), r=["cre"], w=["CTr"])
        S.op('dve', lambda E: E.tensor_scalar(out=CTi[:], in0=pr["cim"][:], scalar1=-1.0, scalar2=None, op0=ALU.mult), r=["cim"], w=["CTi"])
        ph = [sb("ph%d" % i, [128, SEG], F32) for i in range(2)]
        rr = [sb("rr%d" % i, [128, SEG], F32) for i in range(2)]
        cT = sb("cT", [128, SEG], F32); sT = sb("sT", [128, SEG], F32)
        xr = sb("xr", [128, SEG], F32); xi = sb("xi", [128, SEG], F32)
        gr = sb("gr", [128, SEG], F32); gi = sb("gi", [128, SEG], F32)
        hr = sb("hr", [128, SEG], BF16); hi = sb("hi", [128, SEG], BF16)
        carry = sb("carry", [128, 2], F32)
        yst = [sb("yst%d" % i, [32, SEG], F32) for i in range(2)]
        outk = []
        chunks = [(c, min(c + 512, SEG)) for c in range(0, SEG, 512)]
        nseg = LT // SEG
        for pi in range(8):
            d = pi // 4; gp = pi % 4
            fcol = sm["f"][:, pi:pi + 1]; rcol = sm["mag"][:, pi:pi + 1]
            for sg in range(nseg):
                t0 = sg * SEG
                S.op('dve', lambda E: E.tensor_scalar(out=ph[0][:], in0=tI[:, t0:t0 + SEG], scalar1=fcol, scalar2=None, op0=ALU.mult), r=["tI", "f"], w=["ph0"])
                S.op('dve', lambda E: E.tensor_scalar(out=rr[0][:], in0=ph[0][:], scalar1=MAGIC, scalar2=MAGIC, op0=ALU.add, op1=ALU.subtract), r=["ph0"], w=["rr0"])
                S.op('pool', lambda E: E.tensor_tensor(out=ph[0][:], in0=ph[0][:], in1=rr[0][:], op=ALU.subtract), r=["ph0", "rr0"], w=["ph0"])
                S.op('act', lambda E: E.activation(out=sT[:], in_=ph[0][:], func=AF.Sin, scale=TWO_PI_LO), r=["ph0"], w=["sT"])
                S.op('dve', lambda E: E.tensor_scalar(out=ph[1][:], in0=tI[:, t0:t0 + SEG], scalar1=fcol, scalar2=0.25, op0=ALU.mult, op1=ALU.add), r=["tI", "f"], w=["ph1"])
                S.op('dve', lambda E: E.tensor_scalar(out=rr[1][:], in0=ph[1][:], scalar1=MAGIC, scalar2=MAGIC, op0=ALU.add, op1=ALU.subtract), r=["ph1"], w=["rr1"])
                S.op('pool', lambda E: E.tensor_tensor(out=ph[1][:], in0=ph[1][:], in1=rr[1][:], op=ALU.subtract), r=["ph1", "rr1"], w=["ph1"])
                S.op('act', lambda E: E.activation(out=cT[:], in_=ph[1][:], func=AF.Sin, scale=TWO_PI_LO), r=["ph1"], w=["cT"])
                for (c0, c1) in chunks:
                    n = c1 - c0
                    pbr, pbrk = S.ps(); pbi, pbik = S.ps()
                    mm(S, pbr[:, 0:n], BbT[:, pi, 0, :], us[d][:, t0 + c0:t0 + c1], True, True, [("BbT", pi), "us%d" % d], [pbrk])
                    mm(S, pbi[:, 0:n], BbT[:, pi, 1, :], us[d][:, t0 + c0:t0 + c1], True, True, [("BbT", pi), "us%d" % d], [pbik])
                    S.op('dve', lambda E, pbr=pbr, c0=c0, c1=c1, n=n: E.tensor_tensor(out=xr[:, c0:c1], in0=pbr[:, 0:n], in1=cT[:, c0:c1], op=ALU.mult), r=[pbrk, "cT"], w=[("xr", c0)])
                    S.op('dve', lambda E, pbi=pbi, c0=c0, c1=c1, n=n: E.tensor_tensor(out=rr[0][:, c0:c1], in0=pbi[:, 0:n], in1=sT[:, c0:c1], op=ALU.mult), r=[pbik, "sT"], w=["rr0"])
                    S.op('dve', lambda E, pbi=pbi, c0=c0, c1=c1, n=n: E.tensor_tensor(out=xi[:, c0:c1], in0=pbi[:, 0:n], in1=cT[:, c0:c1], op=ALU.mult), r=[pbik, "cT"], w=[("xi", c0)])
                    S.op('dve', lambda E, pbr=pbr, c0=c0, c1=c1, n=n: E.tensor_tensor(out=rr[1][:, c0:c1], in0=pbr[:, 0:n], in1=sT[:, c0:c1], op=ALU.mult), r=[pbrk, "sT"], w=["rr1"])
                    S.op('pool', lambda E, c0=c0, c1=c1: E.tensor_tensor(out=xr[:, c0:c1], in0=xr[:, c0:c1], in1=rr[0][:, c0:c1], op=ALU.add), r=[("xr", c0), "rr0"], w=[("xr", c0)])
                    S.op('pool', lambda E, c0=c0, c1=c1: E.tensor_tensor(out=xi[:, c0:c1], in0=xi[:, c0:c1], in1=rr[1][:, c0:c1], op=ALU.subtract), r=[("xi", c0), "rr1"], w=[("xi", c0)])
                xrk = [("xr", c0) for (c0, c1) in chunks]; xik = [("xi", c0) for (c0, c1) in chunks]
                ini_r = 0.0 if sg == 0 else carry[:, 0:1]; ini_i = 0.0 if sg == 0 else carry[:, 1:2]
                S.op('dve', lambda E, ini_r=ini_r: E.tensor_tensor_scan(out=gr[:], data0=rcol.to_broadcast([128, SEG]), data1=xr[:], initial=ini_r, op0=ALU.mult, op1=ALU.add),
                     r=xrk + ["mag", "carry"], w=["gr"])
                S.op('dve', lambda E, ini_i=ini_i: E.tensor_tensor_scan(out=gi[:], data0=rcol.to_broadcast([128, SEG]), data1=xi[:], initial=ini_i, op0=ALU.mult, op1=ALU.add),
                     r=xik + ["mag", "carry"], w=["gi"])
                S.op('pool', lambda E: E.tensor_copy(out=carry[:, 0:1], in_=gr[:, SEG - 1:SEG]), r=["gr"], w=["carry"])
                S.op('pool', lambda E: E.tensor_copy(out=carry[:, 1:2], in_=gi[:, SEG - 1:SEG]), r=["gi"], w=["carry"])
                S.op('dve', lambda E: E.tensor_tensor(out=rr[0][:], in0=cT[:], in1=gr[:], op=ALU.mult), r=["cT", "gr"], w=["rr0"])
                S.op('pool', lambda E: E.tensor_tensor(out=rr[1][:], in0=sT[:], in1=gi[:], op=ALU.mult), r=["sT", "gi"], w=["rr1"])
                S.op('pool', lambda E: E.tensor_tensor(out=hr[:], in0=rr[0][:], in1=rr[1][:], op=ALU.subtract), r=["rr0", "rr1"], w=["hr"])
                S.op('dve', lambda E: E.tensor_tensor(out=ph[0][:], in0=sT[:], in1=gr[:], op=ALU.mult), r=["sT", "gr"], w=["ph0"])
                S.op('pool', lambda E: E.tensor_tensor(out=ph[1][:], in0=cT[:], in1=gi[:], op=ALU.mult), r=["cT", "gi"], w=["ph1"])
                S.op('pool', lambda E: E.tensor_tensor(out=hi[:], in0=ph[0][:], in1=ph[1][:], op=ALU.add), r=["ph0", "ph1"], w=["hi"])
                ys = yst[(pi * nseg + sg) % 2]
                for (c0, c1) in chunks:
                    n = c1 - c0
                    po, pok = S.ps()
                    mm(S, po[0:32, 0:n], CTr[:, pi, :], hr[:, c0:c1], True, False, ["CTr", "hr"], [pok])
                    mm(S, po[0:32, 0:n], CTi[:, pi, :], hi[:, c0:c1], False, True, ["CTi", "hi"], [pok])
                    S.op('act', lambda E, po=po, ys=ys, c0=c0, c1=c1, n=n: E.activation(out=ys[:, c0:c1], in_=po[0:32, 0:n], func=AF.Identity), r=[pok], w=[ys.name])
                ok = ("o", pi, sg); outk.append(ok)
                dst = ysf if d == 0 else ysb
                S.dma('sp', dst[gp * 32:(gp + 1) * 32, t0:t0 + SEG], ys[:], r=[ys.name], w=[ok])
        S.finish(outk)
    return nc


C1_NT = 5
C1_BLOCKS = [(0, 5)] + [(5 + 4 * i, 9 + 4 * i) for i in range(7)]
GELU_C = 0.044715
GELU_S = 1.5957691216057308


def chan_dft_tables():
    c = np.arange(64)
    a = 2 * np.pi * ((c[:, None] * c[None, :]) % 64) / 64.0
    sc = 1.0 / np.sqrt(8192.0 * 64.0)
    Cc = np.zeros((128, 128)); Sc = np.zeros((128, 128))
    for g in range(2):
        Cc[g * 64:(g + 1) * 64, g * 64:(g + 1) * 64] = np.cos(a) * sc
        Sc[g * 64:(g + 1) * 64, g * 64:(g + 1) * 64] = np.sin(a) * sc
    return np.ascontiguousarray(Cc.astype(np.float32)), np.ascontiguousarray(Sc.astype(np.float32))


def build_phase_c1():
    nc = bass.Bass("TRN2", target_bir_lowering=False)
    dr = common_drams(nc, ffn=False)
    h1 = dram_in(nc, "h1", [TTOK, D]); w_in = dram_in(nc, "w_in", [D, 5120])
    ysfT = dram_in(nc, "ysfT", [256, TTOK]); ysbT = dram_in(nc, "ysbT", [256, TTOK])
    ssm_d = dram_in(nc, "ssm_d", [256]); w_glu = dram_in(nc, "w_glu", [256, 256]); b_glu = dram_in(nc, "b_glu", [256])
    yfrT = dram_in(nc, "yfrT", [256, TTOK], BF16); yfiT = dram_in(nc, "yfiT", [256, TTOK], BF16); ynT = dram_in(nc, "ynT", [512, TTOK], BF16)
    w_bs = dram_in(nc, "w_br_ssm", [256, D]); w_bf = dram_in(nc, "w_br_fnet", [256, D]); w_bn = dram_in(nc, "w_br_na", [512, D])
    w_out = dram_in(nc, "w_out", [D, D])
    tCc = dram_in(nc, "tCc", [128, 128]); tSc = dram_in(nc, "tSc", [128, 128])
    h2 = dram_out(nc, "h2", [TTOK, D])
    with ExitStack() as es:
        S = Sched(nc, es)
        sb = S.sb
        Bf = alloc_tok_bufs(S, NT_MAX=C1_NT, ffn=False)
        C = setup_common(S, nc, dr, ([3, 4], [5], 1.0), [1], Bf)
        NTK = C1_NT * 128
        wbs = sb("wbs", [128, 2, D], BF16); wbn = sb("wbn", [128, 4, D], BF16); wo = sb("wo", [128, 8, D], BF16)
        S.dma('pool', wbs[:], w_bs.rearrange("(a p) d -> p a d", p=128), w=["wbs"])
        S.dma('pool', wbn[:], w_bn.rearrange("(a p) d -> p a d", p=128), w=["wbn"])
        S.dma('pool', wo[:], w_out.rearrange("(a p) d -> p a d", p=128), w=["wo"])
        wglu = sb("wglu", [128, 2, 256], BF16)
        S.dma('pool', wglu[:], w_glu.rearrange("(a p) c -> p a c", p=128), w=["wglu"])
        dT = sb("dT", [128, 2], F32); bgT = sb("bgT", [128, 2], F32)
        S.dma('sp', dT[:], ssm_d.rearrange("(a p) -> p a", p=128), w=["dT"], allow_slow_non_contiguous=True)
        S.dma('sp', bgT[:], b_glu.rearrange("(a p) -> p a", p=128), w=["bgT"], allow_slow_non_contiguous=True)
        tcs = sb("tcs", [128, 2, 128], BF16)
        S.dma('pool', tcs[:, 0, :], tCc, w=["tcs"]); S.dma('pool', tcs[:, 1, :], tSc, w=["tcs"])
        wbf = sb("wbf", [128, 2, D], BF16)
        S.dma('pool', wbf[:], w_bf.rearrange("(a p) d -> p a d", p=128), w=["wbf"])
        Wf = sb("Wf", [128, 4, D], BF16)
        for ri in range(2):
            for a in range(2):
                for hf in range(2):
                    pt, pk = S.ps()
                    mm(S, pt[:], tcs[:, ri, :], wbf[:, a, hf * 512:(hf + 1) * 512], True, True, ["tcs", "wbf"], [pk])
                    S.op('act', lambda E, pt=pt, ri=ri, a=a, hf=hf: E.activation(out=Wf[:, ri * 2 + a, hf * 512:(hf + 1) * 512], in_=pt[:], func=AF.Identity), r=[pk], w=["Wf"])
        wg3 = [sb("wg3_%d" % i, [128, 8, 384], BF16) for i in range(2)]
        yT = sb("yT", [128, 8, NTK], BF16)
        ysT = sb("ysT", [128, 2, NTK], BF16); yfT = sb("yfT", [128, 4, NTK], BF16); ynt = sb("ynt", [128, 4, NTK], BF16)
        ya = sb("ya", [128, 2, 512], F32); yb_ = sb("yb", [128, 2, 512], F32)
        yp = sb("yp", [128, 512], F32); x2 = sb("x2", [128, 512], F32); qq = sb("qq", [128, 512], F32)
        sg_ = sb("sgm", [128, 512], F32); yg = sb("yg", [128, 2, 512], BF16); sg2 = sb("sg2", [128, 512], F32)
        sig = [sb("sig%d" % i, [128, 512], BF16) for i in range(3)]
        prod = [sb("prod%d" % i, [128, 512], F32) for i in range(3)]
        w_in_v = w_in.rearrange("(k p) c -> p k c", p=128)
        outk = []
        ng = 0
        for (t0, t1) in C1_BLOCKS:
            nt = t1 - t0
            chunks = block_chunks(t0, t1)
            tb = t0 * 128
            load_block(S, Bf, h1, t0, t1)
            modulate_block(S, C, Bf, t0, t1, 3, 4)
            ukeys = lambda c0, c1: [("uT", i) for i in range(c0 // 128, (c1 + 127) // 128)]
            S.dma('sp', yfT[:, 0:2, 0:nt * 128], yfrT[:, tb:tb + nt * 128].rearrange("(a p) t -> p a t", p=128), w=["yfT"])
            S.dma('sp', yfT[:, 2:4, 0:nt * 128], yfiT[:, tb:tb + nt * 128].rearrange("(a p) t -> p a t", p=128), w=["yfT"])
            S.dma('sp', ynt[:, :, 0:nt * 128], ynT[:, tb:tb + nt * 128].rearrange("(a p) t -> p a t", p=128), w=["ynt"])
            wus = Bf.wb[Bf.nwb % 4]; Bf.nwb += 1
            S.dma('pool', wus[:], w_in_v[:, :, 0:256], w=[wus.name])
            for (c0, c1, isc) in chunks:
                n = c1 - c0
                S.dma('sp', ya[:, :, 0:n], ysfT[:, tb + c0:tb + c1].rearrange("(a p) t -> p a t", p=128), w=["ya"])
                S.dma('sp', yb_[:, :, 0:n], ysbT[:, tb + c0:tb + c1].rearrange("(a p) t -> p a t", p=128), w=["yb"])
                for a in range(2):
                    pu, puk = S.ps()
                    for k in range(8):
                        mm(S, pu[:, 0:n], wus[:, k, a * 128:(a + 1) * 128], Bf.uT[:, k, c0:c1], k == 0, k == 7, ukeys(c0, c1) + [wus.name], [puk])
                    S.op('pool', lambda E, a=a, n=n: E.tensor_tensor(out=yp[:, 0:n], in0=ya[:, a, 0:n], in1=yb_[:, a, 0:n], op=ALU.add), r=["ya", "yb"], w=["yp"])
                    S.op('dve', lambda E, a=a, n=n, pu=pu: E.scalar_tensor_tensor(out=yp[:, 0:n], in0=pu[:, 0:n], scalar=dT[:, a:a + 1], in1=yp[:, 0:n], op0=ALU.mult, op1=ALU.add),
                         r=[puk, "dT", "yp"], w=["yp"])
                    S.op('pool', lambda E, n=n: E.tensor_tensor(out=x2[:, 0:n], in0=yp[:, 0:n], in1=yp[:, 0:n], op=ALU.mult), r=["yp"], w=["x2"])
                    S.op('dve', lambda E, n=n: E.tensor_scalar(out=x2[:, 0:n], in0=x2[:, 0:n], scalar1=GELU_C, scalar2=1.0, op0=ALU.mult, op1=ALU.add), r=["x2"], w=["x2"])
                    S.op('pool', lambda E, n=n: E.tensor_tensor(out=qq[:, 0:n], in0=x2[:, 0:n], in1=yp[:, 0:n], op=ALU.mult), r=["x2", "yp"], w=["qq"])
                    S.op('act', lambda E, n=n: E.activation(out=sg_[:, 0:n], in_=qq[:, 0:n], func=AF.Sigmoid, scale=GELU_S), r=["qq"], w=["sgm"])
                    S.op('dve', lambda E, a=a, n=n: E.tensor_tensor(out=yg[:, a, 0:n], in0=yp[:, 0:n], in1=sg_[:, 0:n], op=ALU.mult), r=["yp", "sgm"], w=[("yg", a)])
                for co in range(2):
                    pg, pgk = S.ps()
                    for ci in range(2):
                        mm(S, pg[:, 0:n], wglu[:, ci, co * 128:(co + 1) * 128], yg[:, ci, 0:n], ci == 0, ci == 1, ["wglu", ("yg", 0), ("yg", 1)], [pgk])
                    S.op('act', lambda E, pg=pg, co=co, n=n: E.activation(out=sg2[:, 0:n], in_=pg[:, 0:n], func=AF.Sigmoid, bias=bgT[:, co:co + 1], scale=1.0), r=[pgk, "bgT"], w=["sg2"])
                    S.op('dve', lambda E, co=co, n=n, c0=c0, c1=c1: E.tensor_tensor(out=ysT[:, co, c0:c1], in0=yg[:, co, 0:n], in1=sg2[:, 0:n], op=ALU.mult),
                         r=[("yg", co), "sg2"], w=[("ysT", c0)])
            for dc in range(8):
                wg = wg3[ng % 2]; ng += 1
                for gi in range(3):
                    cc = COL_GATE + gi * 1024 + dc * 128
                    S.dma('pool', wg[:, :, gi * 128:(gi + 1) * 128], w_in_v[:, :, cc:cc + 128], w=[(wg.name, gi)])
                for (c0, c1, isc) in chunks:
                    n = c1 - c0
                    pgs = []
                    for gi in range(3):
                        pg, pgk = S.ps(); pgs.append((pg, pgk))
                        for k in range(8):
                            mm(S, pg[:, 0:n], wg[:, k, gi * 128:(gi + 1) * 128], Bf.uT[:, k, c0:c1], k == 0, k == 7, ukeys(c0, c1) + [(wg.name, gi)], [pgk])
                    pbs = []
                    for bi, (wt, wk, src, sk, nk) in enumerate(((wbs, "wbs", ysT, ("ysT", c0), 2), (Wf, "Wf", yfT, "yfT", 4), (wbn, "wbn", ynt, "ynt", 4))):
                        pb, pbk = S.ps(); pbs.append((pb, pbk))
                        for ci in range(nk):
                            mm(S, pb[:, 0:n], wt[:, ci, dc * 128:(dc + 1) * 128], src[:, ci, c0:c1], ci == 0, ci == nk - 1, [wk, sk], [pbk])
                    for gi in range(3):
                        pg, pgk = pgs[gi]; pb, pbk = pbs[gi]
                        S.op('act', lambda E, pg=pg, gi=gi, n=n: E.activation(out=sig[gi][:, 0:n], in_=pg[:, 0:n], func=AF.Sigmoid), r=[pgk], w=[sig[gi].name])
                        S.op('dve', lambda E, pb=pb, gi=gi, n=n: E.tensor_tensor(out=prod[gi][:, 0:n], in0=pb[:, 0:n], in1=sig[gi][:, 0:n], op=ALU.mult),
                             r=[pbk, sig[gi].name], w=[prod[gi].name])
                    S.op('pool', lambda E, n=n: E.tensor_tensor(out=prod[0][:, 0:n], in0=prod[0][:, 0:n], in1=prod[1][:, 0:n], op=ALU.add), r=["prod0", "prod1"], w=["prod0"])
                    S.op('pool', lambda E, n=n, dc=dc, c0=c0, c1=c1: E.tensor_tensor(out=yT[:, dc, c0:c1], in0=prod[0][:, 0:n], in1=prod[2][:, 0:n], op=ALU.add),
                         r=["prod0", "prod2"], w=[("yT", dc, c0)])
            ykeys = [("yT", dc, c0) for dc in range(8) for (c0, c1, isc) in chunks]
            for i in range(nt):
                isc = 1 if (t0 + i == 0) else 0
                mB = C.modB[(5, isc)]
                for hf in range(2):
                    po, pok = S.ps()
                    sl = slice(hf * 512, (hf + 1) * 512)
                    for k in range(8):
                        mm(S, po[:], yT[:, k, i * 128:(i + 1) * 128], wo[:, k, sl], k == 0, k == 7, ykeys + ["wo"], [pok])
                    S.op('dve', lambda E, po=po, mB=mB, sl=sl: E.tensor_tensor(out=Bf.t1[:, sl], in0=po[:], in1=mB[:, sl], op=ALU.mult), r=[pok, mB.name], w=[("t1", hf)])
                    S.op('dve', lambda E, i=i, sl=sl: E.scalar_tensor_tensor(out=Bf.hblk[:, i, sl], in0=Bf.hblk[:, i, sl], scalar=float(DN_ALPHA), in1=Bf.t1[:, sl],
                                                                           op0=ALU.mult, op1=ALU.add), r=[("t1", hf), ("hblk", i)], w=[("hblk", i)])
            post_norm_block(S, C, Bf, nt, 1)
            for i in range(nt):
                ok = ("o_h2", t0 + i); outk.append(ok)
                S.dma('sp', h2[(t0 + i) * 128:(t0 + i + 1) * 128, :], Bf.hblk[:, i, :], r=[("hblk", i)], w=[ok])
        S.finish(outk)
    return nc


_PROGS = {}
DEBUG_HOOK = None
CORES = list(range(8))


def _prog(name, fn):
    if name not in _PROGS:
        _PROGS[name] = fn()
    return _PROGS[name]


def _run(nc, maps):
    return run_bass_kernel_spmd(nc, maps, core_ids=CORES).results


def _common_map(inp, l, b, s=None):
    m = {'c_row': np.ascontiguousarray(inp['c'][b]), 'c_ctx': np.ascontiguousarray(inp['c_ctx']),
         'w_mod': np.ascontiguousarray(inp['w_mod'][l]), 'b_mod': np.ascontiguousarray(inp['b_mod'][l]),
         'ln_g': np.ascontiguousarray(inp['ln_g'][l]), 'ln_b': np.ascontiguousarray(inp['ln_b'][l])}
    if s is not None:
        m['w_gate'] = np.ascontiguousarray(inp['ffn_w_gate'][l, s]); m['w_up'] = np.ascontiguousarray(inp['ffn_w_up'][l, s])
        m['w_down'] = np.ascontiguousarray(inp['ffn_w_down'][l, s])
    return m


def _layer(inp, l, h, hc, last):
    C_ = np.ascontiguousarray
    ra = _run(_prog("A", build_phase_a), phase_a_inputs(inp, l, 0, h, hc))
    if DEBUG_HOOK: DEBUG_HOOK("A", l, ra)
    full = {}
    for nm in ("usT", "kT", "qT"):
        full[nm] = []
        for b in range(NB):
            r0, r1 = ra[2 * b], ra[2 * b + 1]
            full[nm].append(np.concatenate([r0[nm][:, :128], r1[nm][:, :128], r0[nm][:, 128:], r1[nm][:, 128:]], 1))
    for nm in ("v", "zf"):
        full[nm] = []
        for b in range(NB):
            r0, r1 = ra[2 * b], ra[2 * b + 1]
            full[nm].append(np.concatenate([r0[nm][:128], r1[nm][:128], r0[nm][128:], r1[nm][128:]], 0))
    maps = []
    for core in CORES:
        b, j = core_bh(core)
        u = full["usT"][b][j * 128:(j + 1) * 128]
        m = {"usf": C_(u), "usb": C_(np.concatenate([u[:, :LC][:, ::-1], u[:, LC:][:, ::-1]], 1))}
        m.update(s5_host_params(inp, l, j))
        maps.append(m)
    rs = _run(_prog("S5", build_s5), maps)
    if DEBUG_HOOK: DEBUG_HOOK("S5", l, rs)
    tabs = fnet_tables()
    maps = []
    for core in CORES:
        b, j = core_bh(core)
        z = full["zf"][b][:, j * 128:(j + 1) * 128]
        m = {"zf": C_(z[LC:])}
        if not last:
            m["zfc"] = C_(z[:LC])
        for k, v_ in tabs.items():
            m["t_" + k] = v_
        maps.append(m)
    rf = _run(_prog("F%d" % (not last), lambda: build_fnet(not last)), maps)
    if DEBUG_HOOK: DEBUG_HOOK("F", l, rf)
    maps = []
    for core in CORES:
        b, j = core_bh(core)
        rows = slice(j * 256, (j + 1) * 256)
        m = {"qT": C_(full["qT"][b][rows, LC:]), "kT": C_(full["kT"][b][rows, LC:]), "v": C_(full["v"][b][LC:, rows]),
             "kcT": C_(full["kT"][b][rows, :LC]), "vc": C_(full["v"][b][:LC, rows]),
             "bias": na_bias_tables(inp['na_rpb'][l][j * 4:(j + 1) * 4])}
        if not last:
            m["qcT"] = C_(full["qT"][b][rows, :LC])
        maps.append(m)
    rn = _run(_prog("N%d" % (not last), lambda: build_na(not last)), maps)
    if DEBUG_HOOK: DEBUG_HOOK("N", l, rn)
    tCc, tSc = chan_dft_tables()
    maps = []
    for core in CORES:
        b, hf = core_bh(core)
        def sel(x):
            return C_(np.concatenate([x[:, hf * 128:(hf + 1) * 128], x[:, LC + hf * 4096:LC + (hf + 1) * 4096]], 1))
        ysf = np.concatenate([rs[2 * b]["ysf"], rs[2 * b + 1]["ysf"]], 0)
        ysb = np.concatenate([rs[2 * b]["ysb"], rs[2 * b + 1]["ysb"]], 0)
        ysb = np.concatenate([ysb[:, :LC][:, ::-1], ysb[:, LC:][:, ::-1]], 1)
        def fn_full(key, ckey):
            lat = np.concatenate([rf[2 * b][key], rf[2 * b + 1][key]], 1).T
            if last:
                cx = np.zeros((256, LC), lat.dtype)
            else:
                cx = np.concatenate([rf[2 * b][ckey], rf[2 * b + 1][ckey]], 1).T
            return np.concatenate([cx, lat], 1)
        ynl = np.concatenate([rn[2 * b]["ynT"], rn[2 * b + 1]["ynT"]], 0)
        ync = np.zeros((512, LC), ynl.dtype) if last else np.concatenate([rn[2 * b]["yncT"], rn[2 * b + 1]["yncT"]], 0)
        m = _common_map(inp, l, b)
        m.update({"h1": ra[core]["h1"], "w_in": C_(inp['w_in'][l]), "ysfT": sel(ysf), "ysbT": sel(ysb),
                  "ssm_d": C_(inp['ssm_d'][l]), "w_glu": C_(inp['ssm_w_glu'][l]), "b_glu": C_(inp['ssm_b_glu'][l]),
                  "yfrT": sel(fn_full("yfr", "ycr")), "yfiT": sel(fn_full("yfi", "yci")), "ynT": sel(np.concatenate([ync, ynl], 1)),
                  "w_br_ssm": C_(inp['w_br_ssm'][l]), "w_br_fnet": C_(inp['w_br_fnet'][l]), "w_br_na": C_(inp['w_br_na'][l]),
                  "w_out": C_(inp['w_out'][l]), "tCc": tCc, "tSc": tSc})
        maps.append(m)
    rc1 = _run(_prog("C1", build_phase_c1), maps)
    if DEBUG_HOOK: DEBUG_HOOK("C1", l, rc1)
    maps = []
    for core in CORES:
        b, hf = core_bh(core)
        m = _common_map(inp, l, b, 1)
        m["hin"] = rc1[core]["h2"]
        maps.append(m)
    rc2 = _run(_prog("C2", lambda: build_phase_a(6, 7, 8, 2, False)), maps)
    if DEBUG_HOOK: DEBUG_HOOK("C2", l, rc2)
    hn = np.empty_like(h); hcn = np.empty_like(hc)
    for core in CORES:
        b, hf = core_bh(core)
        hcn[b, hf * 128:(hf + 1) * 128] = rc2[core]["h1"][:128]
        hn[b, hf * 4096:(hf + 1) * 4096] = rc2[core]["h1"][128:]
    return hn, hcn


def kernel(**inputs):
    inp = {k: np.asarray(v) for k, v in inputs.items()}
    h = np.ascontiguousarray(inp['x'], dtype=np.float32); hc = np.ascontiguousarray(inp['ctx'], dtype=np.float32)
    for l in range(DEPTH):
        h, hc = _layer(inp, l, h, hc, l == DEPTH - 1)
    return h.astype(np.float32)
```
